# Optimizing a Trainium2 kernel written in Bass

```python
import math
import jax
import jax.numpy as jnp
from jax import lax
import numpy as np

D_MODEL = 1024
BATCH = 2
SEQ = 8192
DEPTH = 2

GRID_W = 64
CTX_LEN = 256
W_BRANCH = 256
N_BRANCH = 4
S5_GROUP = 16
S5_GROUPS = W_BRANCH // S5_GROUP
S5_STATE = 64
FNET_GROUPS = 4
POOL_WINDOWS = (2, 4, 8, 16)
CONV_WIDTH = 31
D_FF = 4 * D_MODEL
D_IN = 5 * W_BRANCH + N_BRANCH * D_MODEL
N_MOD = 6
EPS = 1e-6
POS_BASE = 10000.0

kernel_name = 'hybrid_s5_fnet_pool_conformer_dit'


def _rms_norm(x, g):
    xf = x.astype(jnp.float32)
    y = xf * lax.rsqrt(jnp.mean(xf * xf, axis=-1, keepdims=True) + EPS)
    return (y * g.astype(jnp.float32)).astype(x.dtype)


def _layer_norm(x, g, b):
    xf = x.astype(jnp.float32)
    mu = jnp.mean(xf, axis=-1, keepdims=True)
    xc = xf - mu
    y = xc * lax.rsqrt(jnp.mean(xc * xc, axis=-1, keepdims=True) + EPS)
    return (y * g.astype(jnp.float32) + b.astype(jnp.float32)).astype(x.dtype)


def _modulate(x, g, shift, scale):
    return _rms_norm(x, g) * (1 + scale) + shift


def _pos_embed_2d(rows, dtype):
    row = jnp.repeat(jnp.arange(rows), GRID_W)
    col = jnp.tile(jnp.arange(GRID_W), rows)
    quarter = D_MODEL // 4
    freq = 1.0 / (POS_BASE ** (jnp.arange(quarter, dtype=jnp.float32) / quarter))

    def enc(pos):
        ang = pos.astype(jnp.float32)[:, None] * freq[None, :]
        return jnp.concatenate([jnp.sin(ang), jnp.cos(ang)], axis=-1)

    return jnp.concatenate([enc(row), enc(col)], axis=-1).astype(dtype)


def _s5_discretise(lam_re, lam_im, log_dt, b_re, b_im):
    lam_re = lam_re.astype(jnp.float32)
    lam_im = lam_im.astype(jnp.float32)
    dt = jnp.exp(log_dt.astype(jnp.float32))[:, None]
    mag = jnp.exp(lam_re * dt)
    ang = lam_im * dt
    a_re = mag * jnp.cos(ang)
    a_im = mag * jnp.sin(ang)
    den = lam_re * lam_re + lam_im * lam_im
    f_re = ((a_re - 1) * lam_re + a_im * lam_im) / den
    f_im = (a_im * lam_re - (a_re - 1) * lam_im) / den
    b_re = b_re.astype(jnp.float32)
    b_im = b_im.astype(jnp.float32)
    bb_re = f_re[..., None] * b_re - f_im[..., None] * b_im
    bb_im = f_re[..., None] * b_im + f_im[..., None] * b_re
    return a_re, a_im, bb_re, bb_im


def _complex_affine_combine(e1, e2):
    a1r, a1i, b1r, b1i = e1
    a2r, a2i, b2r, b2i = e2
    return (a2r * a1r - a2i * a1i,
            a2r * a1i + a2i * a1r,
            a2r * b1r - a2i * b1i + b2r,
            a2r * b1i + a2i * b1r + b2i)


def _s5_states(u, lam_re, lam_im, log_dt, b_re, b_im, init):
    bsz, length, _ = u.shape
    uf = u.astype(jnp.float32).reshape(bsz, length, S5_GROUPS, S5_GROUP)
    states = []
    for k, rev in enumerate((False, True)):
        a_re, a_im, bb_re, bb_im = _s5_discretise(lam_re[k], lam_im[k], log_dt[k], b_re[k], b_im[k])
        bu_re = jnp.einsum('blgc,gpc->blgp', uf, bb_re)
        bu_im = jnp.einsum('blgc,gpc->blgp', uf, bb_im)
        if init is not None:
            h0_re, h0_im = init[k]
            first = length - 1 if rev else 0
            bu_re = bu_re.at[:, first].add(a_re * h0_re - a_im * h0_im)
            bu_im = bu_im.at[:, first].add(a_re * h0_im + a_im * h0_re)
        elems = (jnp.broadcast_to(a_re, bu_re.shape), jnp.broadcast_to(a_im, bu_im.shape), bu_re, bu_im)
        _, _, h_re, h_im = lax.associative_scan(_complex_affine_combine, elems, reverse=rev, axis=1)
        states.append((h_re, h_im))
    return states


def _s5_final(states):
    (f_re, f_im), (r_re, r_im) = states
    return [(f_re[:, -1], f_im[:, -1]), (r_re[:, 0], r_im[:, 0])]


def _s5_readout(u, states, c_re, c_im, d, w_glu):
    bsz, length, _ = u.shape
    y = d.astype(jnp.float32) * u.astype(jnp.float32)
    for k in range(2):
        h_re, h_im = states[k]
        y_k = (jnp.einsum('blgp,gcp->blgc', h_re, c_re[k].astype(jnp.float32))
               - jnp.einsum('blgp,gcp->blgc', h_im, c_im[k].astype(jnp.float32)))
        y = y + y_k.reshape(bsz, length, W_BRANCH)
    y = jax.nn.gelu(y).astype(u.dtype)
    return y * jax.nn.sigmoid(y @ w_glu)


def _fnet(u, w):
    bsz, length, width = u.shape
    uf = u.astype(jnp.float32).reshape(bsz, length, FNET_GROUPS, width // FNET_GROUPS)
    y = jnp.fft.fft2(uf, axes=(1, 3), norm='ortho').real.reshape(bsz, length, width)
    return y.astype(u.dtype) @ w


def _multiscale_pool(u, w, scale):
    bsz, length, width = u.shape
    cw = width // len(POOL_WINDOWS)
    uf = u.astype(jnp.float32)
    cs = jnp.concatenate([jnp.zeros((bsz, 1, width), jnp.float32), jnp.cumsum(uf, axis=1)], axis=1)
    t = jnp.arange(length)
    outs = []
    for gi, win in enumerate(POOL_WINDOWS):
        lo = jnp.clip(t - win // 2, 0, length)
        hi = jnp.clip(t + win // 2, 0, length)
        csg = cs[..., gi * cw:(gi + 1) * cw]
        s = jnp.take(csg, hi, axis=1) - jnp.take(csg, lo, axis=1)
        cnt = (hi - lo).astype(jnp.float32)[None, :, None]
        outs.append(s / cnt - uf[..., gi * cw:(gi + 1) * cw])
    y = jnp.stack(outs, axis=2)
    y = jnp.einsum('blgc,gcd->blgd', y, w.astype(jnp.float32)).reshape(bsz, length, width)
    return (y * scale.astype(jnp.float32)).astype(u.dtype)


def _conformer_conv(u_val, u_gate, w_dw, b_dw, ln_g, ln_b, w_pw):
    v = u_val * jax.nn.sigmoid(u_gate)
    y = lax.conv_general_dilated(
        v, w_dw.astype(v.dtype)[:, None, :], window_strides=(1,),
        padding=[(CONV_WIDTH // 2, CONV_WIDTH // 2)],
        dimension_numbers=('NWC', 'WIO', 'NWC'), feature_group_count=v.shape[-1]) + b_dw
    y = jax.nn.silu(_layer_norm(y, ln_g, ln_b))
    return y @ w_pw


def _token_mixer(z, s5_states, p):
    bsz, length, _ = z.shape
    wb = W_BRANCH
    y_s5 = _s5_readout(z[..., 0:wb], s5_states, p['s5_c_re'], p['s5_c_im'], p['s5_d'], p['s5_w_glu'])
    y_fn = _fnet(z[..., wb:2 * wb], p['fnet_w'])
    y_pl = _multiscale_pool(z[..., 2 * wb:3 * wb], p['pool_w'], p['pool_scale'])
    y_cv = _conformer_conv(z[..., 3 * wb:4 * wb], z[..., 4 * wb:5 * wb], p['conv_w'], p['conv_b'],
                           p['conv_ln_g'], p['conv_ln_b'], p['conv_w_out'])
    gates = jax.nn.sigmoid(z[..., 5 * wb:].reshape(bsz, length, N_BRANCH, D_MODEL))
    branches = jnp.stack([y_s5, y_fn, y_pl, y_cv], axis=2)
    proj = jnp.einsum('blkc,kcd->blkd', branches, p['w_branch'])
    merged = jnp.einsum('blkd,blkd->bld', gates, proj)
    return merged @ p['w_out']


def _sq_relu_mlp(h, w1, w2):
    a = jax.nn.relu(h @ w1)
    return (a * a) @ w2


def setup_inputs(seed: int = 0) -> dict:
    key = jax.random.key(seed)
    ks = iter(jax.random.split(key, 40))

    def nrm(shape, scale):
        return jax.random.normal(next(ks), shape, jnp.float32) * scale

    n = jnp.arange(S5_STATE, dtype=jnp.float32)
    s5_shape = (DEPTH, 2, S5_GROUPS, S5_STATE)
    return {
        'x': nrm((BATCH, SEQ, D_MODEL), 1.0),
        'c': nrm((BATCH, D_MODEL), 1.0),
        'ctx': nrm((BATCH, CTX_LEN, D_MODEL), 1.0),
        'c_ctx': nrm((D_MODEL,), 1.0),
        'w_mod': nrm((DEPTH, D_MODEL, N_MOD * D_MODEL), D_MODEL ** -0.5),
        'b_mod': nrm((DEPTH, N_MOD * D_MODEL), 0.02),
        'g_norm1': 1.0 + nrm((DEPTH, D_MODEL), 0.02),
        'w_in': nrm((DEPTH, D_MODEL, D_IN), D_MODEL ** -0.5),
        's5_lam_re': -0.5 + nrm(s5_shape, 0.01),
        's5_lam_im': math.pi * n + nrm(s5_shape, 0.01),
        's5_log_dt': jax.random.uniform(next(ks), (DEPTH, 2, S5_GROUPS), jnp.float32,
                                        math.log(1e-3), math.log(1e-1)),
        's5_b_re': nrm((DEPTH, 2, S5_GROUPS, S5_STATE, S5_GROUP), (2 * S5_GROUP) ** -0.5),
        's5_b_im': nrm((DEPTH, 2, S5_GROUPS, S5_STATE, S5_GROUP), (2 * S5_GROUP) ** -0.5),
        's5_c_re': nrm((DEPTH, 2, S5_GROUPS, S5_GROUP, S5_STATE), S5_STATE ** -0.5),
        's5_c_im': nrm((DEPTH, 2, S5_GROUPS, S5_GROUP, S5_STATE), S5_STATE ** -0.5),
        's5_d': nrm((DEPTH, W_BRANCH), 1.0),
        's5_w_glu': nrm((DEPTH, W_BRANCH, W_BRANCH), W_BRANCH ** -0.5),
        'fnet_w': nrm((DEPTH, W_BRANCH, W_BRANCH), W_BRANCH ** -0.5),
        'pool_w': nrm((DEPTH, len(POOL_WINDOWS), W_BRANCH // 4, W_BRANCH // 4), (W_BRANCH // 4) ** -0.5),
        'pool_scale': 1.0 + nrm((DEPTH, W_BRANCH), 0.02),
        'conv_w': nrm((DEPTH, CONV_WIDTH, W_BRANCH), CONV_WIDTH ** -0.5),
        'conv_b': nrm((DEPTH, W_BRANCH), 0.02),
        'conv_ln_g': 1.0 + nrm((DEPTH, W_BRANCH), 0.02),
        'conv_ln_b': nrm((DEPTH, W_BRANCH), 0.02),
        'conv_w_out': nrm((DEPTH, W_BRANCH, W_BRANCH), W_BRANCH ** -0.5),
        'w_branch': nrm((DEPTH, N_BRANCH, W_BRANCH, D_MODEL), W_BRANCH ** -0.5),
        'w_out': nrm((DEPTH, D_MODEL, D_MODEL), D_MODEL ** -0.5),
        'g_norm2': 1.0 + nrm((DEPTH, D_MODEL), 0.02),
        'mlp_w1': nrm((DEPTH, D_MODEL, D_FF), D_MODEL ** -0.5),
        'mlp_w2': nrm((DEPTH, D_FF, D_MODEL), D_FF ** -0.5),
        'g_final': 1.0 + nrm((D_MODEL,), 0.02),
    }


def reference(x, c, ctx, c_ctx, w_mod, b_mod, g_norm1, w_in, s5_lam_re, s5_lam_im, s5_log_dt,
              s5_b_re, s5_b_im, s5_c_re, s5_c_im, s5_d, s5_w_glu, fnet_w, pool_w, pool_scale,
              conv_w, conv_b, conv_ln_g, conv_ln_b, conv_w_out, w_branch, w_out, g_norm2,
              mlp_w1, mlp_w2, g_final):
    rows = x.shape[1] // GRID_W
    x = x + _pos_embed_2d(rows, x.dtype)[None]
    xc = ctx
    c_act = jax.nn.silu(c)
    cc_act = jax.nn.silu(c_ctx)
    for l in range(DEPTH):
        p = {
            's5_c_re': s5_c_re[l], 's5_c_im': s5_c_im[l], 's5_d': s5_d[l], 's5_w_glu': s5_w_glu[l],
            'fnet_w': fnet_w[l], 'pool_w': pool_w[l], 'pool_scale': pool_scale[l],
            'conv_w': conv_w[l], 'conv_b': conv_b[l], 'conv_ln_g': conv_ln_g[l], 'conv_ln_b': conv_ln_b[l],
            'conv_w_out': conv_w_out[l], 'w_branch': w_branch[l], 'w_out': w_out[l],
        }
        s5p = (s5_lam_re[l], s5_lam_im[l], s5_log_dt[l], s5_b_re[l], s5_b_im[l])
        mod = (c_act @ w_mod[l] + b_mod[l])[:, None, :]
        sh1, sc1, ga1, sh2, sc2, ga2 = jnp.split(mod, N_MOD, axis=-1)
        mod_c = cc_act @ w_mod[l] + b_mod[l]
        csh1, csc1, cga1, csh2, csc2, cga2 = jnp.split(mod_c, N_MOD, axis=-1)

        hc = _modulate(xc, g_norm1[l], csh1, csc1)
        if l == DEPTH - 1:
            ctx_states = _s5_states(hc @ w_in[l][:, :W_BRANCH], *s5p, None)
        else:
            zc = hc @ w_in[l]
            ctx_states = _s5_states(zc[..., :W_BRANCH], *s5p, None)
            xc = xc + cga1 * _token_mixer(zc, ctx_states, p)
            xc = xc + cga2 * _sq_relu_mlp(_modulate(xc, g_norm2[l], csh2, csc2), mlp_w1[l], mlp_w2[l])
        ctx_final = _s5_final(ctx_states)

        h = _modulate(x, g_norm1[l], sh1, sc1)
        z = h @ w_in[l]
        lat_states = _s5_states(z[..., :W_BRANCH], *s5p, ctx_final)
        x = x + ga1 * _token_mixer(z, lat_states, p)
        x = x + ga2 * _sq_relu_mlp(_modulate(x, g_norm2[l], sh2, sc2), mlp_w1[l], mlp_w2[l])
    return _rms_norm(x, g_final)
```

```python
import contextlib
import math
import numpy as np
import concourse.bass as bass
import concourse.mybir as mybir
from concourse.bass_utils import run_bass_kernel_spmd

F32 = mybir.dt.float32
BF16 = mybir.dt.bfloat16
AF = mybir.ActivationFunctionType
ALU = mybir.AluOpType

D = 1024
NT = 2048
CT = 256
L = 8192
NTK = CT + NT
SEQ = CT + L
DEPTH = 2
EPS = 1e-6
TCH = 256
NCORES = 8
TOK = [(0, 256, 1)] + [(256 + 512 * i, 512, 0) for i in range(4)]


class Root:
    __slots__ = ("name", "w", "r")

    def __init__(self, name):
        self.name = name
        self.w = None
        self.r = []


class TV:
    __slots__ = ("ap", "root")

    def __init__(self, ap, root):
        self.ap = ap
        self.root = root

    def __getitem__(self, idx):
        return TV(self.ap[idx], self.root)

    def re(self, ap):
        return TV(ap, self.root)

    def rr(self, pattern_, **kw):
        return TV(self.ap.rearrange(pattern_, **kw), self.root)

    def bc(self, shape):
        return TV(self.ap.to_broadcast(list(shape)), self.root)


class Prog:
    NDMA = {"sp": 6, "act": 2, "pool": 6}

    def __init__(self, nc):
        self.nc = nc
        self.es = contextlib.ExitStack()
        self.eng = {"pe": nc.tensor, "act": nc.scalar, "dve": nc.vector, "pool": nc.gpsimd, "sp": nc.sync}
        self.sems = {}
        self.cnt = {}
        for k in ("pe", "act", "dve", "pool"):
            self.sems[k] = self.es.enter_context(nc.semaphore("s_" + k))
            self.cnt[k] = 0
        self.dring = {}
        for q, n in self.NDMA.items():
            ring = []
            for i in range(n):
                key = "d_%s%d" % (q, i)
                self.sems[key] = self.es.enter_context(nc.semaphore(key))
                self.cnt[key] = 0
                ring.append(key)
            self.dring[q] = [ring, 0]
        self.sems["cc"] = self.es.enter_context(nc.semaphore("s_cc"))
        self.cnt["cc"] = 0
        self.seen = {e: {} for e in self.eng}
        self.n_inst = 0
        self.n_wait = 0
        self._uid = 0
        self.scopes = []
        self.dviews = {}

    def _stack(self):
        return self.scopes[-1] if self.scopes else self.es

    def sbuf(self, name, shape, dt):
        self._uid += 1
        t = self._stack().enter_context(self.nc.sbuf_tensor("%s_%d" % (name, self._uid), list(shape), dt))
        return TV(t.ap(), Root(name))

    def psum(self, name, shape, dt=F32):
        t = self.es.enter_context(self.nc.psum_tensor(name, list(shape), dt))
        return TV(t.ap(), Root(name))

    def dram(self, name, shape, dt, kind="Internal"):
        t = self.nc.dram_tensor(name, list(shape), dt, kind=kind)
        return TV(t.ap(), Root(name))

    def view(self, tv, name=None):
        self._uid += 1
        return TV(tv.ap, Root(name or ("v%d" % self._uid)))

    def dv(self, tv, key, ap=None):
        k = (tv.root.name, key)
        if k not in self.dviews:
            self.dviews[k] = Root("%s/%s" % k)
        return TV(ap if ap is not None else tv.ap, self.dviews[k])

    @contextlib.contextmanager
    def scope(self):
        st = contextlib.ExitStack()
        self.scopes.append(st)
        try:
            yield
        finally:
            self.barrier()
            self.scopes.pop()
            st.close()

    def barrier(self):
        for e in self.eng:
            for key, val in self.cnt.items():
                if val > 0:
                    self._wait(e, (key, val))

    def _wait(self, e, tok):
        if tok is None:
            return
        key, val = tok
        if self.seen[e].get(key, 0) >= val:
            return
        self.eng[e].wait_ge(self.sems[key], val)
        self.seen[e][key] = val
        self.n_wait += 1

    def emit(self, e, fn, reads=(), writes=(), dma=False, cc=False):
        deps = []
        for r in reads:
            if r is not None:
                deps.append(r.root.w)
        for w in writes:
            tok = w.root.w
            if tok is not None and not (e == "pe" and tok[0] == "pe"):
                deps.append(tok)
            for t in w.root.r:
                if not dma and not cc and t[0] == e:
                    continue
                deps.append(t)
        for tok in deps:
            self._wait(e, tok)
        if dma:
            ring, idx = self.dring[e]
            key = ring[idx % len(ring)]
            self.dring[e][1] = idx + 1
            if self.cnt[key] > 0:
                self._wait(e, (key, self.cnt[key]))
            inst = fn(self.eng[e])
            self.cnt[key] += 16
            inst.then_inc(self.sems[key], 16)
            tok = (key, self.cnt[key])
        elif cc:
            inst = fn(self.eng[e])
            self.cnt["cc"] += 1
            inst.then_inc(self.sems["cc"], 1)
            tok = ("cc", self.cnt["cc"])
        else:
            inst = fn(self.eng[e])
            self.cnt[e] += 1
            inst.then_inc(self.sems[e], 1)
            tok = (e, self.cnt[e])
        self.n_inst += 1
        for r in reads:
            if r is not None:
                r.root.r.append(tok)
                if len(r.root.r) > 64:
                    r.root.r = r.root.r[-48:] if False else r.root.r
        for w in writes:
            w.root.w = tok
            w.root.r = []
        return tok

    def dma(self, q, out, in_, **kw):
        return self.emit(q, lambda E: E.dma_start(out=out.ap, in_=in_.ap, **kw), [in_], [out], dma=True)

    def mm(self, out, lhsT, rhs, start=True, stop=True):
        return self.emit("pe", lambda E: E.matmul(out.ap, lhsT.ap, rhs.ap, start=start, stop=stop),
                         [lhsT, rhs], [out])

    def act(self, out, in_, func, scale=1.0, bias=None):
        rd = [in_]
        kw = {}
        if isinstance(scale, TV):
            rd.append(scale)
            kw["scale"] = scale.ap
        else:
            kw["scale"] = scale
        if isinstance(bias, TV):
            rd.append(bias)
            kw["bias"] = bias.ap
        return self.emit("act", lambda E: E.activation(out=out.ap, in_=in_.ap, func=func, **kw), rd, [out])

    def tt(self, e, out, a, b, op):
        return self.emit(e, lambda E: E.tensor_tensor(out=out.ap, in0=a.ap, in1=b.ap, op=op), [a, b], [out])

    def ts(self, e, out, a, s1, op0, s2=None, op1=None):
        rd = [a]
        kw = {}
        s1v = s1
        s2v = s2
        if isinstance(s1, TV):
            rd.append(s1)
            s1v = s1.ap
        if isinstance(s2, TV):
            rd.append(s2)
            s2v = s2.ap
        if op1 is not None:
            kw["op1"] = op1
        return self.emit(e, lambda E: E.tensor_scalar(out=out.ap, in0=a.ap, scalar1=s1v, scalar2=s2v, op0=op0, **kw),
                         rd, [out])

    def stt(self, out, a, s, b, op0, op1):
        rd = [a, b]
        sv = s
        if isinstance(s, TV):
            rd.append(s)
            sv = s.ap
        return self.emit("dve", lambda E: E.scalar_tensor_tensor(out=out.ap, in0=a.ap, scalar=sv, in1=b.ap,
                                                                 op0=op0, op1=op1), rd, [out])

    def copy(self, e, out, in_):
        if e == "act":
            return self.emit(e, lambda E: E.activation(out=out.ap, in_=in_.ap, func=AF.Identity), [in_], [out])
        return self.emit(e, lambda E: E.tensor_copy(out=out.ap, in_=in_.ap), [in_], [out])

    def memset(self, e, out, val):
        return self.emit(e, lambda E: E.memset(out.ap, val), [], [out])

    def recip(self, out, in_):
        return self.emit("dve", lambda E: E.reciprocal(out=out.ap, in_=in_.ap), [in_], [out])

    def scan(self, out, d0, d1, init=0.0):
        return self.emit("dve", lambda E: E.tensor_tensor_scan(out=out.ap, data0=d0.ap, data1=d1.ap, initial=init,
                                                               op0=ALU.mult, op1=ALU.add), [d0, d1], [out])

    def gather(self, in_, out, groups):
        return self.emit("pool", lambda E: E.collective_compute("AllGather", ALU.bypass, replica_groups=groups,
                                                                ins=[in_.ap], outs=[out.ap]), [in_], [out], cc=True)

    def finish(self):
        self.barrier()
        self.es.close()


class Ring:
    def __init__(self, P, name, shape, dt, n):
        self.t = [P.sbuf("%s%d" % (name, i), shape, dt) for i in range(n)]
        self.i = 0

    def next(self):
        t = self.t[self.i % len(self.t)]
        self.i += 1
        return t


def build(dbg=(), upto=None):
    nc = bass.Bass("TRN2", target_bir_lowering=False)
    P = Prog(nc)
    I = {}

    def inp(name, shape, dt=F32):
        I[name] = P.dram(name, shape, dt, kind="ExternalInput")
        return I[name]

    xT = inp("xT", [8, 128, NT]); posT = inp("posT", [8, 128, NT]); ctxT = inp("ctxT", [8, 128, CT])
    cvec = inp("cvec", [128, 8, 2]); msk = inp("msk", [128, 8]); poolfix = inp("poolfix", [128, 16])
    w_mod = inp("w_mod", [DEPTH, D, 6 * D]); b_modT = inp("b_modT", [DEPTH, 128, 48])
    g1T = inp("g1T", [DEPTH, 128, 8]); g2T = inp("g2T", [DEPTH, 128, 8]); gfT = inp("gfT", [128, 8])
    w_in = inp("w_in", [DEPTH, D, 5376]); w_in_own = inp("w_in_own", [DEPTH, D, 384])
    s5lam = inp("s5lam", [DEPTH, 128, 3, 4]); s5BT = inp("s5BT", [DEPTH, 128, 8, 128])
    s5CT = inp("s5CT", [DEPTH, 128, 8, 128]); s5d = inp("s5d", [DEPTH, 128, 1])
    w_small = inp("w_small", [DEPTH, 3, 256, 256])
    poolw = inp("poolw", [DEPTH, 128, 128]); poolsc = inp("poolsc", [DEPTH, 128, 1])
    convw = inp("convw", [DEPTH, 128, 31]); convb = inp("convb", [DEPTH, 128, 1]); lnT = inp("lnT", [DEPTH, 128, 2, 2])
    w_branch = inp("w_branch", [DEPTH, 4, 256, D]); w_out = inp("w_out", [DEPTH, D, D])
    w1 = inp("mlp_w1", [DEPTH, D, 4 * D]); w2 = inp("mlp_w2", [DEPTH, 4 * D, D])
    c_ident = inp("c_ident", [128, 128]); c_cs = inp("c_cs", [128, 128])
    c_w64 = inp("c_w64", [64, 2, 128])
    c_twL = inp("c_twL", [128, 2, 64]); c_twC = inp("c_twC", [4, 2, 64])
    c_fL = inp("c_fL", [128, 2, 128]); c_fC = inp("c_fC", [4, 2, 4])
    out = P.dram("out", [8, 128, NT], F32, kind="ExternalOutput")
    dbg_out = {}

    xres = P.dram("xres", [8, 128, NTK], F32)
    hres = P.dram("hres", [8, 128, NTK], BF16)
    hloc_c = [P.dram("hloc%d" % c, [256, NT], BF16) for c in range(4)]
    hg_c = [P.dram("hg%d" % c, [4 * 256, NT], BF16) for c in range(4)]
    yloc_c = [P.dram("yloc%d" % c, [32, SEQ], BF16) for c in range(8)]
    yg_c = [P.dram("ygc%d" % c, [4 * 32, SEQ], BF16) for c in range(8)]
    zown = P.dram("zown", [3, 128, SEQ], BF16)
    yown = P.dram("yown", [2 * 128, SEQ], BF16)
    ypre = P.dram("ypre", [8, 128, NTK], BF16)
    ypost = P.dram("ypost", [8, 128, NTK], BF16)
    mres = P.dram("mres", [8, 128, NTK], BF16)
    ares = P.dram("ares", [32, 128, NTK], BF16)
    GROUPS = [[0, 1, 2, 3], [4, 5, 6, 7]] if NCORES == 8 else [[0, 1, 2, 3]]

    ps_all = P.psum("ps", [128, 4096])
    banks = [P.view(ps_all[:, 512 * i:512 * (i + 1)], "bank%d" % i) for i in range(8)]
    bstate = [0]

    def bank(lo=0, hi=8):
        i = lo + bstate[0] % (hi - lo)
        bstate[0] += 1
        return banks[i]

    identf = P.sbuf("identf", [128, 128], F32)
    identb = P.sbuf("identb", [128, 128], BF16)
    onesb = P.sbuf("onesb", [128, 128], BF16)
    ones256 = P.sbuf("ones256", [128, 128], F32)
    epsc = P.sbuf("epsc", [128, 1], F32)
    halfpi = P.sbuf("halfpi", [128, 1], F32)
    mskt = P.sbuf("mskt", [128, 8], F32)
    cv = P.sbuf("cv", [128, 8, 2], F32)
    scv = P.sbuf("scv", [128, 8, 2], F32)
    modv = P.sbuf("modv", [128, 48, 2], F32)
    mulv = P.sbuf("mulv", [128, 2, 8, 2], F32)
    gfin = P.sbuf("gfin", [128, 8], F32)
    P.dma("sp", identf, c_ident)
    P.dma("pool", identb, c_ident)
    P.memset("dve", onesb, 1.0 / 1024)
    P.memset("dve", ones256, 1.0 / 256)
    P.memset("dve", epsc, EPS)
    P.memset("dve", halfpi, math.pi / 2)
    P.dma("sp", mskt, msk)
    P.dma("sp", cv, cvec)
    P.dma("sp", gfin, gfT)
    P.act(scv, cv, AF.Silu)

    def tiles_for(layer_last_skip_ctx):
        return [t for t in TOK if not (layer_last_skip_ctx and t[2] == 1)]

    def tile_view(tv, ti, kt=None):
        t0, tn, _ = TOK[ti]
        return P.dv(tv, ti, tv.ap[:, :, t0:t0 + tn])

    with P.scope():
        ra = Ring(P, "s0a", [128, NT], F32, 2)
        rb = Ring(P, "s0b", [128, NT], F32, 2)
        for dt in range(8):
            a = ra.next(); b = rb.next()
            P.dma("sp", a, xT[dt]); P.dma("act", b, posT[dt])
            P.tt("pool" if dt % 2 else "dve", a, a, b, ALU.add)
            P.dma("sp", P.dv(xres, ("s0", dt), xres.ap[dt, :, CT:NTK]), a)
        P.dma("pool", P.dv(xres, "s0c", xres.ap[:, :, 0:CT]), ctxT)

    def norm_stage(l, which, final=False):
        with P.scope():
            xr = Ring(P, "nx", [128, 8, 512], F32, 2)
            sqr = Ring(P, "nsq", [128, 8, 512], BF16, 1)
            tmpr = Ring(P, "ntmp", [128, 8, 512], F32, 1)
            hr = Ring(P, "nh", [128, 8, 512], F32 if final else BF16, 2)
            sdr = Ring(P, "nsd", [128, 512], F32, 2)
            for ti, (t0, tn, s) in enumerate(TOK):
                if final and s == 1:
                    continue
                xt = xr.next(); sq = sqr.next(); tmp = tmpr.next(); h = hr.next(); sd = sdr.next()
                P.dma("sp", xt[:, :, :tn], tile_view(xres, ti).rr("k p t -> p k t"))
                P.act(sq[:, :, :tn], xt[:, :, :tn], AF.Square)
                ps = bank()
                for kt in range(8):
                    P.mm(ps[:, :tn], onesb, sq[:, kt, :tn], start=kt == 0, stop=kt == 7)
                P.act(sd[:, :tn], ps[:, :tn], AF.Sqrt, bias=epsc)
                P.recip(sd[:, :tn], sd[:, :tn])
                P.tt("dve", tmp[:, :, :tn], xt[:, :, :tn], sd[:, None, :tn].bc([128, 8, tn]), ALU.mult)
                for kt in range(8):
                    if final:
                        P.act(h[:, kt, :tn], tmp[:, kt, :tn], AF.Identity, scale=gfin[:, kt:kt + 1])
                    else:
                        P.act(h[:, kt, :tn], tmp[:, kt, :tn], AF.Identity, scale=mulv[:, which, kt, s:s + 1],
                              bias=modv[:, (0 if which == 0 else 24) + kt, s:s + 1])
                if final:
                    P.dma("sp", P.dv(out, ti, out.ap[:, :, t0 - CT:t0 - CT + tn]).rr("k p t -> p k t"), h[:, :, :tn])
                else:
                    P.dma("pool", tile_view(hres, ti).rr("k p t -> p k t"), h[:, :, :tn])

    def mod_stage(l):
        with P.scope():
            wr = Ring(P, "wm", [128, 8, 512], F32, 2)
            bm = P.sbuf("bm", [128, 48], F32)
            g12 = P.sbuf("g12", [128, 2, 8], F32)
            P.dma("sp", bm, b_modT[l]); P.dma("sp", g12[:, 0, :], g1T[l]); P.dma("sp", g12[:, 1, :], g2T[l])
            ps = bank()
            wv = w_mod.ap[l].rearrange("(kt p) m -> p kt m", p=128)
            for ch in range(12):
                w = wr.next()
                P.dma("sp" if ch % 2 else "act", w, P.dv(w_mod, (l, ch), wv[:, :, 512 * ch:512 * (ch + 1)]))
                for o in range(4):
                    oc = ch * 4 + o
                    for kt in range(8):
                        P.mm(ps[:, 2 * oc:2 * oc + 2], w[:, kt, 128 * o:128 * (o + 1)], scv[:, kt, :],
                             start=kt == 0, stop=kt == 7)
            P.tt("dve", modv, ps[:, 0:96].rr("p (o s) -> p o s", s=2), bm[:, :, None].bc([128, 48, 2]), ALU.add)
            for which, base in ((0, 8), (1, 32)):
                P.ts("dve", mulv[:, which], modv[:, base:base + 8, :], 1.0, ALU.add)
                P.tt("dve", mulv[:, which], mulv[:, which], g12[:, which, :, None].bc([128, 8, 2]), ALU.mult)

    def dense_stage(name, wview, KT, M, src, tiles, evac, pre=None, post=None, src_fn=None):
        with P.scope():
            wsb = P.sbuf("w_" + name, [128, KT, M], BF16)
            cw = 512 if KT <= 8 else 128
            for c0 in range(0, M, cw):
                P.dma("pool", P.view(wsb[:, :, c0:c0 + cw]), TV(wview[:, :, c0:c0 + cw], Root("wsrc")))
            wsb = P.view(wsb)
            P.barrier()
            xr = Ring(P, "dx_" + name, [128, KT, 512], BF16, 2)
            for ti in tiles:
                t0, tn, s = TOK[ti]
                xin = xr.next()
                sv = src_fn(ti) if src_fn else tile_view(src, ti).rr("k p t -> p k t")
                P.dma("sp", xin[:, :, :tn], sv)
                st = pre(ti) if pre else None
                for mt in range(M // 128):
                    ps = bank(0, 6)
                    for kt in range(KT):
                        P.mm(ps[:, :tn], wsb[:, kt, 128 * mt:128 * (mt + 1)], xin[:, kt, :tn],
                             start=kt == 0, stop=kt == KT - 1)
                    evac(st, mt, ti, ps[:, :tn])
                if post:
                    post(st, ti)

    def zown_stage(l):
        with P.scope():
            for c in range(4):
                P.dma("sp" if c % 2 else "act", P.dv(hloc_c[c], "all", hloc_c[c].ap.rearrange("(k p) t -> k p t", p=128)),
                      P.dv(hres, ("lat", c), hres.ap[2 * c:2 * c + 2, :, CT:NTK]))
        with P.scope():
            for c in range(4):
                P.gather(hloc_c[c], hg_c[c], GROUPS)
        with P.scope():
            wsb = P.sbuf("w_zo", [128, 8, 384], BF16)
            P.dma("pool", wsb, P.dv(w_in_own, l, w_in_own.ap[l].rearrange("(kt p) m -> p kt m", p=128)))
            xr = Ring(P, "zx", [128, 8, 512], BF16, 2)
            zr = Ring(P, "zz", [128, 3, 512], BF16, 2)
            hgv = [hg_c[c].ap.rearrange("(r k p) t -> r p k t", r=4, k=2) for c in range(4)]
            for si in range(17):
                xin = xr.next(); z = zr.next()
                if si == 0:
                    tn = CT; s0 = 0
                    P.dma("sp", xin[:, :, :tn], tile_view(hres, 0).rr("k p t -> p k t"))
                else:
                    tn = 512; r = (si - 1) // 4; tt = (si - 1) % 4; s0 = CT + (si - 1) * 512
                    for c in range(4):
                        P.dma("sp" if c % 2 else "act", xin[:, 2 * c:2 * c + 2, :],
                              P.dv(hg_c[c], si, hgv[c][r, :, :, 512 * tt:512 * (tt + 1)]))
                for mt in range(3):
                    ps = bank()
                    for kt in range(8):
                        P.mm(ps[:, :tn], wsb[:, kt, 128 * mt:128 * (mt + 1)], xin[:, kt, :tn], start=kt == 0, stop=kt == 7)
                    if mt == 2:
                        P.act(z[:, mt, :tn], ps[:, :tn], AF.Sigmoid)
                    elif mt == 1:
                        P.copy("dve", z[:, mt, :tn], ps[:, :tn])
                    else:
                        P.copy("act", z[:, mt, :tn], ps[:, :tn])
                P.dma("pool", P.dv(zown, si, zown.ap[:, :, s0:s0 + tn]).rr("m p t -> p m t"), z[:, :, :tn])

    def s5_stage(l):
        T = TCH
        with P.scope():
            za = P.sbuf("za", [128, SEQ], BF16)
            P.dma("sp", za, P.dv(zown, "za", zown.ap[0]))
            lam = P.sbuf("lam", [128, 3, 4], F32)
            P.dma("sp", lam, s5lam[l])
            BT = P.sbuf("BT", [128, 8, 128], BF16)
            CTp = P.sbuf("CTp", [128, 8, 128], BF16)
            CTn = P.sbuf("CTn", [128, 8, 128], BF16)
            P.dma("pool", BT, s5BT[l]); P.dma("pool", CTp, s5CT[l])
            P.ts("pool", CTn, CTp, -1.0, ALU.mult)
            dcol = P.sbuf("dcol", [128, 1], F32)
            P.dma("sp", dcol, s5d[l])
            sm = {n: P.sbuf("s5_" + n, [128, 4], F32) for n in
                  ("dt", "mag", "ang", "c", "s", "t1", "t2", "are", "aim", "den", "fre", "fim", "ar1",
                   "etr", "eti", "inr", "ini", "u1", "u2", "u3", "u4")}
            P.act(sm["dt"], lam[:, 2, :], AF.Exp)
            P.tt("dve", sm["t1"], lam[:, 0, :], sm["dt"], ALU.mult)
            P.act(sm["mag"], sm["t1"], AF.Exp)
            P.tt("dve", sm["ang"], lam[:, 1, :], sm["dt"], ALU.mult)
            P.act(sm["s"], sm["ang"], AF.Sin, scale=1.0 / 32)
            P.act(sm["c"], sm["ang"], AF.Sin, scale=1.0 / 32, bias=halfpi)

            def csq(c, s):
                P.tt("dve", sm["t1"], c, c, ALU.mult)
                P.tt("dve", sm["t2"], s, s, ALU.mult)
                P.stt(s, c, 2.0, s, ALU.mult, ALU.mult)
                P.tt("dve", c, sm["t1"], sm["t2"], ALU.subtract)
            for _ in range(5):
                csq(sm["c"], sm["s"])
            P.tt("dve", sm["are"], sm["mag"], sm["c"], ALU.mult)
            P.tt("dve", sm["aim"], sm["mag"], sm["s"], ALU.mult)
            P.tt("dve", sm["t1"], lam[:, 0, :], lam[:, 0, :], ALU.mult)
            P.tt("dve", sm["t2"], lam[:, 1, :], lam[:, 1, :], ALU.mult)
            P.tt("dve", sm["den"], sm["t1"], sm["t2"], ALU.add)
            P.recip(sm["den"], sm["den"])
            P.ts("dve", sm["ar1"], sm["are"], -1.0, ALU.add)
            P.tt("dve", sm["t1"], sm["ar1"], lam[:, 0, :], ALU.mult)
            P.tt("dve", sm["t2"], sm["aim"], lam[:, 1, :], ALU.mult)
            P.tt("dve", sm["fre"], sm["t1"], sm["t2"], ALU.add)
            P.tt("dve", sm["fre"], sm["fre"], sm["den"], ALU.mult)
            P.tt("dve", sm["t1"], sm["aim"], lam[:, 0, :], ALU.mult)
            P.tt("dve", sm["t2"], sm["ar1"], lam[:, 1, :], ALU.mult)
            P.tt("dve", sm["fim"], sm["t1"], sm["t2"], ALU.subtract)
            P.tt("dve", sm["fim"], sm["fim"], sm["den"], ALU.mult)
            Er = P.sbuf("Er", [128, 4, T], F32); Ei = P.sbuf("Ei", [128, 4, T], F32)
            Rr = P.sbuf("Rr", [128, 4, T], F32); Ri = P.sbuf("Ri", [128, 4, T], F32)
            MAG0 = P.sbuf("MAG0", [128, 4, T], F32)
            tA = P.sbuf("tA", [128, 4, T], F32); tB = P.sbuf("tB", [128, 4, T], F32)
            P.memset("dve", Er[:, :, 0:1], 1.0); P.memset("dve", Ei[:, :, 0:1], 0.0)
            pc = P.sbuf("pc", [128, 4], F32); psn = P.sbuf("psn", [128, 4], F32)
            P.copy("dve", pc, sm["c"]); P.copy("dve", psn, sm["s"])
            s = 1
            while s < T:
                cb = pc[:, :, None].bc([128, 4, s]); sb = psn[:, :, None].bc([128, 4, s])
                P.tt("dve", tA[:, :, 0:s], Er[:, :, 0:s], cb, ALU.mult)
                P.tt("dve", tB[:, :, 0:s], Ei[:, :, 0:s], sb, ALU.mult)
                P.tt("dve", Er[:, :, s:2 * s], tA[:, :, 0:s], tB[:, :, 0:s], ALU.subtract)
                P.tt("dve", tA[:, :, 0:s], Er[:, :, 0:s], sb, ALU.mult)
                P.tt("dve", tB[:, :, 0:s], Ei[:, :, 0:s], cb, ALU.mult)
                P.tt("dve", Ei[:, :, s:2 * s], tA[:, :, 0:s], tB[:, :, 0:s], ALU.add)
                csq(pc, psn)
                s *= 2
            P.tt("dve", sm["etr"], pc, sm["mag"], ALU.mult)
            P.tt("dve", sm["eti"], psn, sm["mag"], ALU.mult)
            frb = sm["fre"][:, :, None].bc([128, 4, T]); fib = sm["fim"][:, :, None].bc([128, 4, T])
            P.tt("dve", tA, Er, frb, ALU.mult); P.tt("dve", tB, Ei, fib, ALU.mult)
            P.tt("dve", Rr, tA, tB, ALU.add)
            P.tt("dve", tA, Er, fib, ALU.mult); P.tt("dve", tB, Ei, frb, ALU.mult)
            P.tt("dve", Ri, tA, tB, ALU.subtract)
            P.copy("dve", MAG0, sm["mag"][:, :, None].bc([128, 4, T]))
            P.memset("dve", MAG0[:, :, 0:1], 0.0)
            P.memset("dve", sm["inr"], 0.0); P.memset("dve", sm["ini"], 0.0)

            ysum = P.sbuf("ysum", [128, SEQ], F32)
            bus = P.sbuf("bus", [128, 2, 4, T], F32)
            t1 = P.sbuf("t1", [128, 4, T], F32); t2 = P.sbuf("t2", [128, 4, T], F32)
            t3 = P.sbuf("t3", [128, 4, T], F32); t4 = P.sbuf("t4", [128, 4, T], F32)
            mre = P.sbuf("mre", [128, 4, T], F32); mim = P.sbuf("mim", [128, 4, T], F32)
            pr = [P.sbuf("pr%d" % i, [128, 4, T], BF16) for i in range(4)]
            BU = P.view(ps_all[:, 0:2048], "BU")
            ybk = [P.view(ps_all[:, 2048 + 512 * i:2048 + 512 * (i + 1)], "yb%d" % i) for i in range(4)]
            ycnt = [0]
            for (start, nch) in ((0, CT // T), (CT, L // T)):
                for ci in range(nch):
                    cf = start + ci * T
                    cbk = start + (nch - 1 - ci) * T
                    buv = BU.rr("p (r k t) -> p r k t", r=2, k=4)
                    for kq in range(4):
                        k = kq // 2
                        if k == 0:
                            rhs = za[:, cf:cf + T]
                        else:
                            rhs = za.re(za.ap[:, cbk:cbk + T][:, ::-1])
                        for ri in range(2):
                            P.mm(buv[:, ri, kq, :], BT[:, kq * 2 + ri, :], rhs)
                    P.copy("act", bus, buv)
                    P.tt("pool", t1, Rr, bus[:, 0], ALU.mult)
                    P.tt("pool", t2, Ri, bus[:, 1], ALU.mult)
                    P.tt("pool", t1, t1, t2, ALU.subtract)
                    P.tt("dve", t3, Rr, bus[:, 1], ALU.mult)
                    P.tt("dve", t4, Ri, bus[:, 0], ALU.mult)
                    P.tt("pool", t3, t3, t4, ALU.add)
                    P.tt("dve", t1[:, :, 0], t1[:, :, 0], sm["inr"], ALU.add)
                    P.tt("dve", t3[:, :, 0], t3[:, :, 0], sm["ini"], ALU.add)
                    P.scan(mre.rr("p k t -> p (k t)"), MAG0.rr("p k t -> p (k t)"), t1.rr("p k t -> p (k t)"))
                    P.scan(mim.rr("p k t -> p (k t)"), MAG0.rr("p k t -> p (k t)"), t3.rr("p k t -> p (k t)"))
                    lr = mre[:, :, T - 1]; li = mim[:, :, T - 1]
                    P.tt("dve", sm["u1"], sm["etr"], lr, ALU.mult)
                    P.tt("dve", sm["u2"], sm["eti"], li, ALU.mult)
                    P.tt("dve", sm["inr"], sm["u1"], sm["u2"], ALU.subtract)
                    P.tt("dve", sm["u3"], sm["eti"], lr, ALU.mult)
                    P.tt("dve", sm["u4"], sm["etr"], li, ALU.mult)
                    P.tt("dve", sm["ini"], sm["u3"], sm["u4"], ALU.add)
                    P.tt("pool", pr[0], Er, mre, ALU.mult)
                    P.tt("dve", pr[1], Ei, mim, ALU.mult)
                    P.tt("pool", pr[2], Ei, mre, ALU.mult)
                    P.tt("dve", pr[3], Er, mim, ALU.mult)
                    for k in range(2):
                        yb = ybk[ycnt[0] % 4]; ycnt[0] += 1
                        n = 0
                        for q in range(2):
                            kq = 2 * k + q
                            for pi, (tab, ri) in enumerate(((CTp, 0), (CTn, 0), (CTn, 1), (CTn, 1))):
                                rhs = pr[pi][:, kq, :]
                                if k == 1:
                                    rhs = pr[pi].re(pr[pi].ap[:, kq, :][:, ::-1])
                                P.mm(yb[:, :T], tab[:, kq * 2 + ri, :], rhs, start=n == 0, stop=n == 7)
                                n += 1
                        c0 = cf if k == 0 else cbk
                        first = (ci < nch - 1 - ci) if k == 0 else (nch - 1 - ci > ci)
                        if nch == 1:
                            first = (k == 0)
                        elif ci == nch - 1 - ci:
                            first = False
                        if first:
                            P.copy("act", ysum[:, c0:c0 + T], yb[:, :T])
                        else:
                            P.tt("dve", ysum[:, c0:c0 + T], ysum[:, c0:c0 + T], yb[:, :T], ALU.add)
            ya = P.sbuf("ya", [128, SEQ // 4], F32); yb2 = P.sbuf("yb2", [128, SEQ // 4], F32)
            yo = P.sbuf("yo", [128, SEQ // 4], BF16)
            W = SEQ // 4
            for c in range(4):
                sl = slice(c * W, (c + 1) * W)
                P.stt(ya, za[:, sl], dcol[:, 0:1], ysum[:, sl], ALU.mult, ALU.add)
                P.tt("pool", yb2, ya, ya, ALU.mult)
                P.ts("pool", yb2, yb2, 0.044715, ALU.mult, 1.0, ALU.add)
                P.tt("pool", yb2, yb2, ya, ALU.mult)
                P.act(yb2, yb2, AF.Sigmoid, scale=2.0 * math.sqrt(2.0 / math.pi))
                P.tt("dve", yo, ya, yb2, ALU.mult)
                P.dma("sp", P.dv(yown, ("s5", c), yown.ap[0:64, sl]), yo[0:64, :])

    def fnet_stage(l):
        with P.scope():
            za = P.sbuf("za", [128, SEQ], BF16)
            P.dma("sp", za, P.dv(zown, "za", zown.ap[0]))
            cs = P.sbuf("cs", [128, 128], BF16); P.dma("pool", cs, c_cs)
            w64 = P.sbuf("w64", [64, 2, 128], BF16); P.dma("pool", w64, c_w64)
            for (start, TF, ctw, cf) in ((0, 4, c_twC, c_fC), (CT, 128, c_twL, c_fL)):
              with P.scope():
                Lq = 64 * TF
                tw = P.sbuf("tw%d" % TF, [TF, 2, 64], F32); P.dma("sp", tw, ctw)
                fb = P.sbuf("fb%d" % TF, [TF, 2, TF], BF16); P.dma("pool", fb, cf)
                X = P.sbuf("X%d" % TF, [64, TF, 128], BF16)
                A = P.sbuf("A%d" % TF, [TF, 64, 128], F32)
                Ar = P.sbuf("Ar%d" % TF, [TF, 64, 64], BF16); Ai = P.sbuf("Ai%d" % TF, [TF, 64, 64], BF16)
                u1 = P.sbuf("u1%d" % TF, [TF, 64, 64], F32); u2 = P.sbuf("u2%d" % TF, [TF, 64, 64], F32)
                Y2 = P.sbuf("Y2%d" % TF, [TF, 64, 128], BF16)
                yo = P.sbuf("fyo%d" % TF, [128, Lq], BF16)
                P.memset("pool", Y2, 0.0)
                for g0 in range(0, TF, 4):
                    ps = bank()
                    for tf in range(g0, g0 + 4):
                        lhs = za.re(za.ap[:, start + tf:start + tf + TF * 63 + 1:TF])
                        P.mm(ps[0:64, 128 * (tf - g0):128 * (tf - g0 + 1)], lhs, cs)
                    P.copy("act" if (g0 // 4) % 2 else "dve", X[:, g0:g0 + 4, :], ps[0:64, :].rr("p (a c) -> p a c", a=4))
                for m0 in range(0, 64, 4):
                    ps = bank()
                    for m in range(m0, m0 + 4):
                        o = ps[0:TF, 128 * (m - m0):128 * (m - m0 + 1)]
                        P.mm(o, X[:, :, m], w64[:, 0, :], start=True, stop=False)
                        P.mm(o, X[:, :, 64 + m], w64[:, 1, :], start=False, stop=True)
                    P.copy("act" if (m0 // 4) % 2 else "dve", A[:, m0:m0 + 4, :], ps[0:TF, :].rr("p (a c) -> p a c", a=4))
                twr = tw[:, 0:1, :].bc([TF, 64, 64]); twi = tw[:, 1:2, :].bc([TF, 64, 64])
                P.tt("dve", u1, A[:, :, 0:64], twr, ALU.mult)
                P.tt("pool", u2, A[:, :, 64:128], twi, ALU.mult)
                P.tt("dve", Ar, u1, u2, ALU.add)
                P.tt("dve", u1, A[:, :, 0:64], twi, ALU.mult)
                P.tt("pool", u2, A[:, :, 64:128], twr, ALU.mult)
                P.tt("dve", Ai, u1, u2, ALU.subtract)
                scale = 1.0 / math.sqrt(64.0 * Lq)
                for blk in range(8):
                    ps = bank()
                    P.mm(ps[0:TF, :], fb[:, 0, :], Ar[:, 8 * blk:8 * blk + 8, :].rr("p m k -> p (m k)"), start=True, stop=False)
                    P.mm(ps[0:TF, :], fb[:, 1, :], Ai[:, 8 * blk:8 * blk + 8, :].rr("p m k -> p (m k)"), start=False, stop=True)
                    P.act(Y2[:, :, 64 + 8 * blk:64 + 8 * blk + 8].rr("p k m -> p m k"),
                          ps[0:TF, :].rr("p (m k) -> p m k", m=8), AF.Identity, scale=scale)
                yov = yo.ap.rearrange("p (kb ka) -> p ka kb", ka=64)
                per = max(1, 512 // TF)
                for ka0 in range(0, 64, per):
                    ps = bank()
                    nk = min(per, 64 - ka0)
                    for ka in range(ka0, ka0 + nk):
                        P.mm(ps[:, TF * (ka - ka0):TF * (ka - ka0 + 1)], Y2[:, ka, :], identb[0:TF, 0:TF])
                    P.copy("act" if (ka0 // per) % 2 else "dve", yo.re(yov[64:128, ka0:ka0 + nk, :]),
                           ps[64:128, 0:nk * TF].rr("p (a b) -> p a b", a=nk))
                P.dma("sp", P.dv(yown, ("fn", TF), yown.ap[64:128, start:start + Lq]), yo[64:128, :])

    def poolconv_stage(l):
        with P.scope():
            zb = P.sbuf("zb", [128, SEQ], BF16); zc = P.sbuf("zc", [128, SEQ], BF16)
            P.dma("sp", zb, P.dv(zown, "zb", zown.ap[1])); P.dma("sp", zc, P.dv(zown, "zc", zown.ap[2]))
            pw = P.sbuf("pw", [128, 128], BF16); P.dma("pool", pw, poolw[l])
            psc = P.sbuf("psc", [128, 1], F32); P.dma("sp", psc, poolsc[l])
            cw = P.sbuf("cw", [128, 31], F32); P.dma("sp", cw, convw[l])
            cb = P.sbuf("cb", [128, 1], F32); P.dma("sp", cb, convb[l])
            pfx = P.sbuf("pfx", [128, 16], F32); P.dma("sp", pfx, poolfix)
            Dk = P.sbuf("Dk", [128, 31, 128], BF16)
            for k in range(31):
                P.ts("pool", Dk[:, k, :], identf, cw[:, k:k + 1], ALU.mult)
            vx = P.sbuf("vx", [128, SEQ + 64], BF16)
            P.memset("pool", vx, 0.0)
            offs = {0: 16, CT: 16 + CT + 32}
            P.tt("dve", vx[:, 16:16 + CT], zb[:, 0:CT], zc[:, 0:CT], ALU.mult)
            P.tt("dve", vx[:, offs[CT]:offs[CT] + L], zb[:, CT:SEQ], zc[:, CT:SEQ], ALU.mult)
            yo = Ring(P, "cyo", [128, 512], BF16, 2)
            for si in range(17):
                if si == 0:
                    s0 = 0; tn = CT; e0 = 16
                else:
                    s0 = CT + (si - 1) * 512; tn = 512; e0 = offs[CT] + (si - 1) * 512
                ps = bank()
                for k in range(31):
                    P.mm(ps[:, :tn], Dk[:, k, :], vx[:, e0 + k - 15:e0 + k - 15 + tn], start=k == 0, stop=k == 30)
                y = yo.next()
                P.act(y[:, :tn], ps[:, :tn], AF.Identity, bias=cb[:, 0:1])
                P.dma("sp", P.dv(yown, ("cv", si), yown.ap[128 + 64:256, s0:s0 + tn]), y[64:128, :tn])
            CH = 2048
            ue = P.sbuf("ue", [128, CH + 32], F32)
            sa = P.sbuf("sa", [128, CH + 32], F32); sb_ = P.sbuf("sb", [128, CH + 32], F32)
            acc = P.sbuf("acc", [128, CH], F32)
            yp = P.sbuf("yp", [128, CH], BF16)
            po = Ring(P, "po", [128, 512], BF16, 2)
            P.memset("pool", yp, 0.0)
            for (start, Lq) in ((0, CT), (CT, L)):
                nchk = max(1, Lq // CH)
                cl = min(CH, Lq)
                for c in range(nchk):
                    c0 = start + c * cl
                    lo = 16 if c == 0 else 0
                    hi = 16 if c == nchk - 1 else 0
                    if lo or hi:
                        P.memset("pool", ue, 0.0)
                    P.copy("pool", ue[0:64, lo:cl + 32 - hi], zb[0:64, c0 - 16 + lo:c0 + cl + 16 - hi])
                    n = cl + 32
                    P.tt("pool", sa[0:64, 0:n - 1], ue[0:64, 0:n - 1], ue[0:64, 1:n], ALU.add)
                    P.ts("dve", acc[0:64, :cl], sa[0:64, 15:15 + cl], mskt[0:64, 4:5], ALU.mult)
                    P.tt("pool", sb_[0:64, 0:n - 3], sa[0:64, 0:n - 3], sa[0:64, 2:n - 1], ALU.add)
                    P.stt(acc[0:64, :cl], sb_[0:64, 14:14 + cl], mskt[0:64, 5:6], acc[0:64, :cl], ALU.mult, ALU.add)
                    P.tt("pool", sa[0:64, 0:n - 7], sb_[0:64, 0:n - 7], sb_[0:64, 4:n - 3], ALU.add)
                    P.stt(acc[0:64, :cl], sa[0:64, 12:12 + cl], mskt[0:64, 6:7], acc[0:64, :cl], ALU.mult, ALU.add)
                    P.tt("pool", sb_[0:64, 0:n - 15], sa[0:64, 0:n - 15], sa[0:64, 8:n - 7], ALU.add)
                    P.stt(acc[0:64, :cl], sb_[0:64, 8:8 + cl], mskt[0:64, 7:8], acc[0:64, :cl], ALU.mult, ALU.add)
                    if c == 0:
                        P.tt("dve", acc[0:64, 0:8], acc[0:64, 0:8], pfx[0:64, 0:8], ALU.mult)
                    if c == nchk - 1:
                        P.tt("dve", acc[0:64, cl - 8:cl], acc[0:64, cl - 8:cl], pfx[0:64, 8:16], ALU.mult)
                    P.tt("dve", yp[0:64, :cl], acc[0:64, :cl], ue[0:64, 16:16 + cl], ALU.subtract)
                    for t in range(0, cl, 512):
                        tn = min(512, cl - t)
                        ps = bank()
                        P.mm(ps[:, :tn], pw, yp[:, t:t + tn])
                        o = po.next()
                        P.act(o[0:64, :tn], ps[0:64, :tn], AF.Identity, scale=psc[0:64, 0:1])
                        P.dma("sp", P.dv(yown, ("pl", c0 + t), yown.ap[128:128 + 64, c0 + t:c0 + t + tn]), o[0:64, :tn])

    def select_stage(l):
        with P.scope():
            for c in range(8):
                P.dma("sp" if c % 2 else "act", yloc_c[c], P.dv(yown, ("cp", c), yown.ap[32 * c:32 * c + 32, :]))
        with P.scope():
            for c in range(8):
                P.gather(yloc_c[c], yg_c[c], GROUPS)
        with P.scope():
            ygv = [yg_c[c].ap.rearrange("(r i) s -> r i s", r=4) for c in range(8)]
            cr = Ring(P, "cand", [128, 4, NT], BF16, 2)
            orr = Ring(P, "selo", [128, NT], BF16, 2)
            tmp = P.sbuf("seltmp", [128, NT], F32)
            for br in range(4):
                T_ = 0 if br < 2 else 1
                ph = 0 if br in (0, 2) else 64
                for ct in range(2):
                    cand = cr.next(); o = orr.next()
                    for rl in range(2):
                        r = 2 * ct + rl
                        for hh in range(2):
                            c = (T_ * 128 + ph) // 32 + hh
                            p0 = 64 * rl + 32 * hh
                            P.dma("sp" if hh else "act", cand[p0:p0 + 32],
                                  P.dv(yg_c[c], (br, ct, rl), ygv[c][r, :, CT:SEQ]).rr("p (g t) -> p g t", g=4))
                            P.dma("pool", P.dv(ypre, ("c", br, ct, rl, hh), ypre.ap[br * 2 + ct, p0:p0 + 32, 0:CT]),
                                  P.dv(yg_c[c], ("c", br, ct, rl), ygv[c][r, :, 0:CT]))
                    P.ts("dve", tmp, cand[:, 0, :], mskt[:, 0:1], ALU.mult)
                    P.stt(tmp, cand[:, 1, :], mskt[:, 1:2], tmp, ALU.mult, ALU.add)
                    P.stt(tmp, cand[:, 2, :], mskt[:, 2:3], tmp, ALU.mult, ALU.add)
                    P.stt(o, cand[:, 3, :], mskt[:, 3:4], tmp, ALU.mult, ALU.add)
                    P.dma("pool", P.dv(ypre, ("l", br, ct), ypre.ap[br * 2 + ct, :, CT:NTK]), o)

    def post_stage(l, tiles):
        with P.scope():
            ws = P.sbuf("wsmall", [128, 3, 2, 256], BF16)
            P.dma("pool", ws, P.dv(w_small, l, w_small.ap[l].rearrange("w (kt p) m -> p w kt m", p=128)))
            ln = P.sbuf("ln", [128, 2, 2], F32); P.dma("sp", ln, lnT[l])
            yr = Ring(P, "py", [128, 8, 512], BF16, 2)
            orr = Ring(P, "pout", [128, 8, 512], BF16, 2)
            sg = P.sbuf("psg", [128, 512], F32)
            ycf = P.sbuf("ycf", [128, 2, 512], F32); xc = P.sbuf("xc", [128, 2, 512], F32)
            sq = P.sbuf("psq", [128, 2, 512], F32); sd = P.sbuf("psd", [128, 512], F32)
            yl = P.sbuf("yl", [128, 2, 512], BF16)
            for ti in tiles:
                t0, tn, s = TOK[ti]
                y = yr.next(); o = orr.next()
                P.dma("sp", y[:, :, :tn], tile_view(ypre, ti).rr("k p t -> p k t"))
                for mt in range(2):
                    ps = bank()
                    for kt in range(2):
                        P.mm(ps[:, :tn], ws[:, 0, kt, 128 * mt:128 * mt + 128], y[:, kt, :tn], start=kt == 0, stop=kt == 1)
                    P.act(sg[:, :tn], ps[:, :tn], AF.Sigmoid)
                    P.tt("dve", o[:, mt, :tn], y[:, mt, :tn], sg[:, :tn], ALU.mult)
                for mt in range(2):
                    ps = bank()
                    for kt in range(2):
                        P.mm(ps[:, :tn], ws[:, 1, kt, 128 * mt:128 * mt + 128], y[:, 2 + kt, :tn], start=kt == 0, stop=kt == 1)
                    P.copy("act", o[:, 2 + mt, :tn], ps[:, :tn])
                P.copy("pool", o[:, 4:6, :tn], y[:, 4:6, :tn])
                P.copy("pool", ycf[:, :, :tn], y[:, 6:8, :tn])
                ps = bank()
                for kt in range(2):
                    P.mm(ps[:, :tn], ones256, ycf[:, kt, :tn], start=kt == 0, stop=kt == 1)
                for kt in range(2):
                    P.tt("dve", xc[:, kt, :tn], ycf[:, kt, :tn], ps[:, :tn], ALU.subtract)
                P.act(sq[:, :, :tn], xc[:, :, :tn], AF.Square)
                ps2 = bank()
                for kt in range(2):
                    P.mm(ps2[:, :tn], ones256, sq[:, kt, :tn], start=kt == 0, stop=kt == 1)
                P.act(sd[:, :tn], ps2[:, :tn], AF.Sqrt, bias=epsc)
                P.recip(sd[:, :tn], sd[:, :tn])
                P.tt("dve", xc[:, :, :tn], xc[:, :, :tn], sd[:, None, :tn].bc([128, 2, tn]), ALU.mult)
                for kt in range(2):
                    P.act(yl[:, kt, :tn], xc[:, kt, :tn], AF.Silu, scale=ln[:, kt, 0:1], bias=ln[:, kt, 1:2])
                for mt in range(2):
                    ps = bank()
                    for kt in range(2):
                        P.mm(ps[:, :tn], ws[:, 2, kt, 128 * mt:128 * mt + 128], yl[:, kt, :tn], start=kt == 0, stop=kt == 1)
                    P.copy("act", o[:, 6 + mt, :tn], ps[:, :tn])
                P.dma("pool", tile_view(ypost, ti).rr("k p t -> p k t"), o[:, :, :tn])

    def merge_stage(l, tiles):
        wb = [None]
        yr = [None]
        accr = [None]
        mo = [None]

        def pre(ti):
            t0, tn, s = TOK[ti]
            if wb[0] is None:
                wb[0] = P.sbuf("wbr", [128, 4, 2, D], BF16)
                P.dma("pool", wb[0], P.dv(w_branch, l, w_branch.ap[l].rearrange("b (kt p) m -> p b kt m", p=128)))
                yr[0] = Ring(P, "my", [128, 8, 512], BF16, 2)
                accr[0] = [P.sbuf("macc", [128, 512], F32), P.sbuf("mtmp", [128, 512], F32), P.sbuf("msg", [128, 512], F32)]
                mo[0] = Ring(P, "mo", [128, 8, 512], BF16, 2)
            y = yr[0].next()
            P.dma("act", y[:, :, :tn], tile_view(ypost, ti).rr("k p t -> p k t"))
            return {"y": y, "o": mo[0].next()}

        def evac(st, mt, ti, ps):
            t0, tn, s = TOK[ti]
            dt, br = mt // 4, mt % 4
            acc, tmp, sg = accr[0]
            pp = bank(6, 8)
            for kt in range(2):
                P.mm(pp[:, :tn], wb[0][:, br, kt, 128 * dt:128 * dt + 128], st["y"][:, 2 * br + kt, :tn], start=kt == 0, stop=kt == 1)
            P.act(sg[:, :tn], ps, AF.Sigmoid)
            if br == 0:
                P.tt("dve", acc[:, :tn], pp[:, :tn], sg[:, :tn], ALU.mult)
            else:
                P.tt("dve", tmp[:, :tn], pp[:, :tn], sg[:, :tn], ALU.mult)
                if br < 3:
                    P.tt("pool", acc[:, :tn], acc[:, :tn], tmp[:, :tn], ALU.add)
                else:
                    P.tt("pool", st["o"][:, dt, :tn], acc[:, :tn], tmp[:, :tn], ALU.add)

        def post(st, ti):
            t0, tn, s = TOK[ti]
            P.dma("pool", tile_view(mres, ti).rr("k p t -> p k t"), st["o"][:, :, :tn])

        wv = w_in.ap[l].rearrange("(kt p) m -> p kt m", p=128)[:, :, 1280:5376]
        wv = wv.rearrange("p kt (b d c) -> p kt d b c", b=4, d=8)
        with P.scope():
            wsb = P.sbuf("w_gate", [128, 8, 8, 4, 128], BF16)
            for dt in range(8):
                for br in range(4):
                    P.dma("pool", P.view(wsb[:, :, dt, br]), TV(wv[:, :, dt, br], Root("wsrc")))
            wsb2 = P.view(wsb.rr("p kt d b c -> p kt (d b c)"))
            P.barrier()
            xr = Ring(P, "dx_gate", [128, 8, 512], BF16, 2)
            for ti in tiles:
                t0, tn, s = TOK[ti]
                xin = xr.next()
                P.dma("sp", xin[:, :, :tn], tile_view(hres, ti).rr("k p t -> p k t"))
                st = pre(ti)
                for mt in range(32):
                    ps = bank(0, 6)
                    for kt in range(8):
                        P.mm(ps[:, :tn], wsb2[:, kt, 128 * mt:128 * (mt + 1)], xin[:, kt, :tn], start=kt == 0, stop=kt == 7)
                    evac(st, mt, ti, ps[:, :tn])
                post(st, ti)

    def resid_stage(name, wview, KT, src, tiles, gate_base):
        xr = [None]

        def pre(ti):
            t0, tn, s = TOK[ti]
            if xr[0] is None:
                xr[0] = Ring(P, "rx_" + name, [128, 8, 512], F32, 2)
            x = xr[0].next()
            P.dma("act", x[:, :, :tn], tile_view(xres, ti).rr("k p t -> p k t"))
            return x

        def evac(x, mt, ti, ps):
            t0, tn, s = TOK[ti]
            P.stt(x[:, mt, :tn], ps, modv[:, gate_base + mt, s:s + 1], x[:, mt, :tn], ALU.mult, ALU.add)

        def post(x, ti):
            t0, tn, s = TOK[ti]
            P.dma("pool", tile_view(xres, ti).rr("k p t -> p k t"), x[:, :, :tn])
        dense_stage(name, wview, KT, D, src, tiles, evac, pre, post)

    def mlp1_stage(l, tiles):
        ar = [None]

        def pre(ti):
            if ar[0] is None:
                ar[0] = (Ring(P, "m1a", [128, 32, 512], BF16, 2), Ring(P, "m1r", [128, 512], F32, 3))
            return ar[0][0].next()

        def evac(a, mt, ti, ps):
            t0, tn, s = TOK[ti]
            r = ar[0][1].next()
            P.act(r[:, :tn], ps, AF.Relu)
            P.tt("pool", a[:, mt, :tn], r[:, :tn], r[:, :tn], ALU.mult)

        def post(a, ti):
            t0, tn, s = TOK[ti]
            P.dma("pool", tile_view(ares, ti).rr("k p t -> p k t"), a[:, :, :tn])
        dense_stage("w1", w1.ap[l].rearrange("(kt p) m -> p kt m", p=128), 8, 4 * D, hres, tiles, evac, pre, post)

    sched = []
    for l in range(DEPTH):
        last = l == DEPTH - 1
        tl = [i for i, t in enumerate(TOK) if not (last and t[2] == 1)]
        sched += [
            ("mod%d" % l, lambda l=l: mod_stage(l)),
            ("norm1_%d" % l, lambda l=l: norm_stage(l, 0)),
            ("zown%d" % l, lambda l=l: zown_stage(l)),
            ("s5_%d" % l, lambda l=l: s5_stage(l)),
            ("fnet%d" % l, lambda l=l: fnet_stage(l)),
            ("poolconv%d" % l, lambda l=l: poolconv_stage(l)),
            ("select%d" % l, lambda l=l: select_stage(l)),
            ("post%d" % l, lambda l=l, tl=tl: post_stage(l, tl)),
            ("merge%d" % l, lambda l=l, tl=tl: merge_stage(l, tl)),
            ("wo%d" % l, lambda l=l, tl=tl: resid_stage("wo", w_out.ap[l].rearrange("(kt p) m -> p kt m", p=128), 8, mres, tl, 16)),
            ("norm2_%d" % l, lambda l=l: norm_stage(l, 1)),
            ("mlp1_%d" % l, lambda l=l, tl=tl: mlp1_stage(l, tl)),
            ("w2_%d" % l, lambda l=l, tl=tl: resid_stage("w2", w2.ap[l].rearrange("(kt p) m -> p kt m", p=128), 32, ares, tl, 40)),
        ]
    sched.append(("final", lambda: norm_stage(DEPTH - 1, 0, final=True)))
    scratch = dict(xres=xres, hres=hres, zown=zown, yown=yown, ypre=ypre, ypost=ypost, mres=mres, ares=ares)
    for name, fn in sched:
        fn()
        if upto is not None and name == upto:
            break
    if dbg:
        with P.scope():
            for nm in dbg:
                if nm == "modv":
                    d = P.dram("dbg_modv", [128, 96], F32, kind="ExternalOutput")
                    P.dma("sp", d, modv.rr("p o s -> p (o s)"))
                    continue
                src = scratch[nm]
                shp = list(src.ap.shape)
                d = P.dram("dbg_" + nm, shp, src.ap.dtype, kind="ExternalOutput")
                P.dma("sp", d, P.dv(src, "dbg"))
    P.finish()
    return nc, P


def _consts():
    c = {}
    c["c_ident"] = np.eye(128, dtype=np.float32)
    j = np.arange(64)
    m = np.arange(64)
    cs = np.zeros((128, 128), np.float32)
    ang = 2 * np.pi * np.outer(j, m) / 64
    cs[64:, 0:64] = np.cos(ang)
    cs[64:, 64:128] = np.sin(ang)
    c["c_cs"] = cs
    ts = np.arange(64); ka = np.arange(64)
    a = 2 * np.pi * np.outer(ts, ka) / 64
    w64 = np.zeros((64, 2, 128), np.float32)
    w64[:, 0, 0:64] = np.cos(a); w64[:, 0, 64:] = np.sin(a)
    w64[:, 1, 0:64] = -np.sin(a); w64[:, 1, 64:] = np.cos(a)
    c["c_w64"] = w64
    for nm, TF in (("L", 128), ("C", 4)):
        tf = np.arange(TF)
        ph = 2 * np.pi * np.outer(tf, ka) / (64 * TF)
        tw = np.zeros((TF, 2, 64), np.float32)
        tw[:, 0] = np.cos(ph); tw[:, 1] = -np.sin(ph)
        c["c_tw" + nm] = tw
        a2 = 2 * np.pi * np.outer(tf, np.arange(TF)) / TF
        f = np.zeros((TF, 2, TF), np.float32)
        f[:, 0] = np.cos(a2); f[:, 1] = np.sin(a2)
        c["c_f" + nm] = f
    return c


def _pos_table():
    rows = L // 64
    row = np.repeat(np.arange(rows), 64).astype(np.float32)
    col = np.tile(np.arange(64), rows).astype(np.float32)
    q = D // 4
    freq = (1.0 / (10000.0 ** (np.arange(q, dtype=np.float32) / q))).astype(np.float32)

    def enc(p):
        a = (p[:, None] * freq[None, :]).astype(np.float32)
        return np.concatenate([np.sin(a), np.cos(a)], axis=-1)
    return np.concatenate([enc(row), enc(col)], axis=-1).astype(np.float32)


def _fm(a):
    return np.ascontiguousarray(a.T.reshape(8, 128, a.shape[0]))


def _col(v):
    return np.ascontiguousarray(v.reshape(-1, 128).T)


def make_inputs(inp):
    f = lambda k: np.asarray(inp[k], dtype=np.float32)
    x, c, ctx, c_ctx = f("x"), f("c"), f("ctx"), f("c_ctx")
    pos = _pos_table()
    consts = _consts()
    shared = dict(consts)
    shared["w_mod"] = f("w_mod")
    shared["b_modT"] = np.stack([_col(f("b_mod")[l]) for l in range(DEPTH)])
    shared["g1T"] = np.stack([_col(f("g_norm1")[l]) for l in range(DEPTH)])
    shared["g2T"] = np.stack([_col(f("g_norm2")[l]) for l in range(DEPTH)])
    shared["gfT"] = _col(f("g_final"))
    shared["w_in"] = f("w_in")
    shared["w_small"] = np.stack([np.stack([f("s5_w_glu")[l], f("fnet_w")[l], f("conv_w_out")[l]]) for l in range(DEPTH)])
    shared["lnT"] = np.stack([np.stack([_col(f("conv_ln_g")[l]), _col(f("conv_ln_b")[l])], axis=-1) for l in range(DEPTH)])
    shared["w_branch"] = f("w_branch"); shared["w_out"] = f("w_out")
    shared["mlp_w1"] = f("mlp_w1"); shared["mlp_w2"] = f("mlp_w2")
    win = f("w_in")
    lam_re, lam_im, log_dt = f("s5_lam_re"), f("s5_lam_im"), f("s5_log_dt")
    b_re, b_im, c_re, c_im = f("s5_b_re"), f("s5_b_im"), f("s5_c_re"), f("s5_c_im")
    maps = []
    for core in range(8):
        b, j = core // 4, core % 4
        m = dict(shared)
        m["xT"] = _fm(x[b, NT * j:NT * (j + 1)])
        m["posT"] = _fm(pos[NT * j:NT * (j + 1)])
        m["ctxT"] = _fm(ctx[b])
        m["cvec"] = np.ascontiguousarray(np.stack([_col(c[b]), _col(c_ctx)], axis=-1))
        mk = np.zeros((128, 8), np.float32)
        mk[:, j] = 1.0
        wj = [2, 4, 8, 16][j]
        mk[:, 4 + j] = 1.0 / wj
        m["msk"] = mk
        pf = np.ones((128, 16), np.float32)
        for t in range(8):
            lo = max(t - wj // 2, 0); hi = t + wj // 2
            pf[:, t] = wj / float(hi - lo)
            e = 8 - t
            cnt = min(wj // 2, e) + wj // 2
            pf[:, 8 + t] = wj / float(cnt)
        m["poolfix"] = pf
        wo = np.zeros((DEPTH, D, 384), np.float32)
        for l in range(DEPTH):
            wo[l, :, 0:64] = win[l][:, 64 * j:64 * j + 64]
            wo[l, :, 64:128] = win[l][:, 256 + 64 * j:256 + 64 * j + 64]
            wo[l, :, 128:192] = win[l][:, 512 + 64 * j:512 + 64 * j + 64]
            wo[l, :, 192:256] = win[l][:, 768 + 64 * j:768 + 64 * j + 64]
            wo[l, :, 320:384] = win[l][:, 1024 + 64 * j:1024 + 64 * j + 64]
        m["w_in_own"] = wo
        lamt = np.zeros((DEPTH, 128, 3, 4), np.float32)
        BT = np.zeros((DEPTH, 128, 8, 128), np.float32)
        CTt = np.zeros((DEPTH, 128, 8, 128), np.float32)
        dd = np.zeros((DEPTH, 128, 1), np.float32)
        for l in range(DEPTH):
            dd[l, 0:64, 0] = f("s5_d")[l][64 * j:64 * j + 64]
            for k in range(2):
                for q in range(2):
                    kq = 2 * k + q
                    for gl in range(2):
                        g = 4 * j + 2 * q + gl
                        pr = slice(64 * gl, 64 * gl + 64)
                        lamt[l, pr, 0, kq] = lam_re[l, k, g]
                        lamt[l, pr, 1, kq] = lam_im[l, k, g]
                        lamt[l, pr, 2, kq] = log_dt[l, k, g]
                        ch = slice(32 * q + 16 * gl, 32 * q + 16 * gl + 16)
                        BT[l, ch, kq * 2 + 0, pr] = b_re[l, k, g].T
                        BT[l, ch, kq * 2 + 1, pr] = b_im[l, k, g].T
                        CTt[l, pr, kq * 2 + 0, ch] = c_re[l, k, g].T
                        CTt[l, pr, kq * 2 + 1, ch] = c_im[l, k, g].T
        m["s5lam"] = lamt; m["s5BT"] = BT; m["s5CT"] = CTt; m["s5d"] = dd
        pw = np.zeros((DEPTH, 128, 128), np.float32)
        psc = np.zeros((DEPTH, 128, 1), np.float32)
        cwt = np.zeros((DEPTH, 128, 31), np.float32)
        cbt = np.zeros((DEPTH, 128, 1), np.float32)
        for l in range(DEPTH):
            pw[l, 0:64, 0:64] = f("pool_w")[l, j]
            psc[l, 0:64, 0] = f("pool_scale")[l][64 * j:64 * j + 64]
            cwt[l, 64:128, :] = f("conv_w")[l][:, 64 * j:64 * j + 64].T
            cbt[l, 64:128, 0] = f("conv_b")[l][64 * j:64 * j + 64]
        m["poolw"] = pw; m["poolsc"] = psc; m["convw"] = cwt; m["convb"] = cbt
        maps.append({k: np.ascontiguousarray(v, dtype=np.float32) for k, v in m.items()})
    return maps


_NC = [None]


def kernel(**inputs):
    if _NC[0] is None:
        _NC[0] = build()[0]
    nc = _NC[0]
    maps = make_inputs(inputs)
    res = run_bass_kernel_spmd(nc, maps, core_ids=list(range(8)))
    outp = np.zeros((2, L, D), np.float32)
    for core in range(8):
        b, j = core // 4, core % 4
        o = np.asarray(res.results[core]["out"], dtype=np.float32)
        outp[b, NT * j:NT * (j + 1)] = o.reshape(D, NT).T
    return outp
```

```python
import contextlib
import math
import numpy as np
import concourse.bass as bass
import concourse.mybir as mybir
from concourse.bass_utils import run_bass_kernel_spmd

F32 = mybir.dt.float32
BF16 = mybir.dt.bfloat16
AF = mybir.ActivationFunctionType
ALU = mybir.AluOpType

D = 1024
NT = 2048
CT = 256
L = 8192
NTK = CT + NT
SEQ = CT + L
DEPTH = 2
EPS = 1e-6
TCH = 256
NCORES = 8
TOK = [(0, 256, 1)] + [(256 + 512 * i, 512, 0) for i in range(4)]


class Root:
    __slots__ = ("name", "w", "r")

    def __init__(self, name):
        self.name = name
        self.w = None
        self.r = []


class TV:
    __slots__ = ("ap", "root")

    def __init__(self, ap, root):
        self.ap = ap
        self.root = root

    def __getitem__(self, idx):
        return TV(self.ap[idx], self.root)

    def re(self, ap):
        return TV(ap, self.root)

    def rr(self, pattern_, **kw):
        return TV(self.ap.rearrange(pattern_, **kw), self.root)

    def bc(self, shape):
        return TV(self.ap.to_broadcast(list(shape)), self.root)


class Prog:
    NDMA = {"sp": 6, "act": 2, "pool": 6}

    def __init__(self, nc):
        self.nc = nc
        self.es = contextlib.ExitStack()
        self.eng = {"pe": nc.tensor, "act": nc.scalar, "dve": nc.vector, "pool": nc.gpsimd, "sp": nc.sync}
        self.sems = {}
        self.cnt = {}
        for k in ("pe", "act", "dve", "pool"):
            self.sems[k] = self.es.enter_context(nc.semaphore("s_" + k))
            self.cnt[k] = 0
        self.dring = {}
        for q, n in self.NDMA.items():
            ring = []
            for i in range(n):
                key = "d_%s%d" % (q, i)
                self.sems[key] = self.es.enter_context(nc.semaphore(key))
                self.cnt[key] = 0
                ring.append(key)
            self.dring[q] = [ring, 0]
        self.sems["cc"] = self.es.enter_context(nc.semaphore("s_cc"))
        self.cnt["cc"] = 0
        self.seen = {e: {} for e in self.eng}
        self.n_inst = 0
        self.n_wait = 0
        self._uid = 0
        self.scopes = []
        self.dviews = {}

    def _stack(self):
        return self.scopes[-1] if self.scopes else self.es

    def sbuf(self, name, shape, dt):
        self._uid += 1
        t = self._stack().enter_context(self.nc.sbuf_tensor("%s_%d" % (name, self._uid), list(shape), dt))
        return TV(t.ap(), Root(name))

    def psum(self, name, shape, dt=F32):
        t = self.es.enter_context(self.nc.psum_tensor(name, list(shape), dt))
        return TV(t.ap(), Root(name))

    def dram(self, name, shape, dt, kind="Internal"):
        t = self.nc.dram_tensor(name, list(shape), dt, kind=kind)
        return TV(t.ap(), Root(name))

    def view(self, tv, name=None):
        self._uid += 1
        return TV(tv.ap, Root(name or ("v%d" % self._uid)))

    def dv(self, tv, key, ap=None):
        k = (tv.root.name, key)
        if k not in self.dviews:
            self.dviews[k] = Root("%s/%s" % k)
        return TV(ap if ap is not None else tv.ap, self.dviews[k])

    @contextlib.contextmanager
    def scope(self):
        st = contextlib.ExitStack()
        self.scopes.append(st)
        try:
            yield
        finally:
            self.barrier()
            self.scopes.pop()
            st.close()

    def barrier(self):
        for e in self.eng:
            for key, val in self.cnt.items():
                if val > 0:
                    self._wait(e, (key, val))

    def _wait(self, e, tok):
        if tok is None:
            return
        key, val = tok
        if self.seen[e].get(key, 0) >= val:
            return
        self.eng[e].wait_ge(self.sems[key], val)
        self.seen[e][key] = val
        self.n_wait += 1

    def emit(self, e, fn, reads=(), writes=(), dma=False, cc=False):
        deps = []
        for r in reads:
            if r is not None:
                deps.append(r.root.w)
        for w in writes:
            tok = w.root.w
            if tok is not None and not (e == "pe" and tok[0] == "pe"):
                deps.append(tok)
            for t in w.root.r:
                if not dma and not cc and t[0] == e:
                    continue
                deps.append(t)
        for tok in deps:
            self._wait(e, tok)
        if dma:
            ring, idx = self.dring[e]
            key = ring[idx % len(ring)]
            self.dring[e][1] = idx + 1
            if self.cnt[key] > 0:
                self._wait(e, (key, self.cnt[key]))
            inst = fn(self.eng[e])
            self.cnt[key] += 16
            inst.then_inc(self.sems[key], 16)
            tok = (key, self.cnt[key])
        elif cc:
            inst = fn(self.eng[e])
            self.cnt["cc"] += 1
            inst.then_inc(self.sems["cc"], 1)
            tok = ("cc", self.cnt["cc"])
        else:
            inst = fn(self.eng[e])
            self.cnt[e] += 1
            inst.then_inc(self.sems[e], 1)
            tok = (e, self.cnt[e])
        self.n_inst += 1
        for r in reads:
            if r is not None:
                r.root.r.append(tok)
                if len(r.root.r) > 64:
                    r.root.r = r.root.r[-48:] if False else r.root.r
        for w in writes:
            w.root.w = tok
            w.root.r = []
        return tok

    def dma(self, q, out, in_, **kw):
        return self.emit(q, lambda E: E.dma_start(out=out.ap, in_=in_.ap, **kw), [in_], [out], dma=True)

    def mm(self, out, lhsT, rhs, start=True, stop=True):
        return self.emit("pe", lambda E: E.matmul(out.ap, lhsT.ap, rhs.ap, start=start, stop=stop),
                         [lhsT, rhs], [out])

    def act(self, out, in_, func, scale=1.0, bias=None):
        rd = [in_]
        kw = {}
        if isinstance(scale, TV):
            rd.append(scale)
            kw["scale"] = scale.ap
        else:
            kw["scale"] = scale
        if isinstance(bias, TV):
            rd.append(bias)
            kw["bias"] = bias.ap
        return self.emit("act", lambda E: E.activation(out=out.ap, in_=in_.ap, func=func, **kw), rd, [out])

    def tt(self, e, out, a, b, op):
        return self.emit(e, lambda E: E.tensor_tensor(out=out.ap, in0=a.ap, in1=b.ap, op=op), [a, b], [out])

    def ts(self, e, out, a, s1, op0, s2=None, op1=None):
        rd = [a]
        kw = {}
        s1v = s1
        s2v = s2
        if isinstance(s1, TV):
            rd.append(s1)
            s1v = s1.ap
        if isinstance(s2, TV):
            rd.append(s2)
            s2v = s2.ap
        if op1 is not None:
            kw["op1"] = op1
        return self.emit(e, lambda E: E.tensor_scalar(out=out.ap, in0=a.ap, scalar1=s1v, scalar2=s2v, op0=op0, **kw),
                         rd, [out])

    def stt(self, out, a, s, b, op0, op1):
        rd = [a, b]
        sv = s
        if isinstance(s, TV):
            rd.append(s)
            sv = s.ap
        return self.emit("dve", lambda E: E.scalar_tensor_tensor(out=out.ap, in0=a.ap, scalar=sv, in1=b.ap,
                                                                 op0=op0, op1=op1), rd, [out])

    def copy(self, e, out, in_):
        if e == "act":
            return self.emit(e, lambda E: E.activation(out=out.ap, in_=in_.ap, func=AF.Identity), [in_], [out])
        return self.emit(e, lambda E: E.tensor_copy(out=out.ap, in_=in_.ap), [in_], [out])

    def memset(self, e, out, val):
        return self.emit(e, lambda E: E.memset(out.ap, val), [], [out])

    def recip(self, out, in_):
        return self.emit("dve", lambda E: E.reciprocal(out=out.ap, in_=in_.ap), [in_], [out])

    def scan(self, out, d0, d1, init=0.0):
        return self.emit("dve", lambda E: E.tensor_tensor_scan(out=out.ap, data0=d0.ap, data1=d1.ap, initial=init,
                                                               op0=ALU.mult, op1=ALU.add), [d0, d1], [out])

    def gather(self, in_, out, groups):
        return self.emit("pool", lambda E: E.collective_compute("AllGather", ALU.bypass, replica_groups=groups,
                                                                ins=[in_.ap], outs=[out.ap]), [in_], [out], cc=True)

    def finish(self):
        self.barrier()
        self.es.close()


class Ring:
    def __init__(self, P, name, shape, dt, n):
        self.t = [P.sbuf("%s%d" % (name, i), shape, dt) for i in range(n)]
        self.i = 0

    def next(self):
        t = self.t[self.i % len(self.t)]
        self.i += 1
        return t


def build(dbg=(), upto=None):
    nc = bass.Bass("TRN2", target_bir_lowering=False)
    P = Prog(nc)
    I = {}

    def inp(name, shape, dt=F32):
        I[name] = P.dram(name, shape, dt, kind="ExternalInput")
        return I[name]

    xT = inp("xT", [8, 128, NT]); posT = inp("posT", [8, 128, NT]); ctxT = inp("ctxT", [8, 128, CT])
    cvec = inp("cvec", [128, 8, 2]); msk = inp("msk", [128, 8]); poolfix = inp("poolfix", [128, 16])
    w_mod = inp("w_mod", [DEPTH, D, 6 * D]); b_modT = inp("b_modT", [DEPTH, 128, 48])
    g1T = inp("g1T", [DEPTH, 128, 8]); g2T = inp("g2T", [DEPTH, 128, 8]); gfT = inp("gfT", [128, 8])
    w_in = inp("w_in", [DEPTH, D, 5376]); w_in_own = inp("w_in_own", [DEPTH, D, 384])
    s5lam = inp("s5lam", [DEPTH, 128, 3, 4]); s5BT = inp("s5BT", [DEPTH, 128, 8, 128])
    s5CT = inp("s5CT", [DEPTH, 128, 8, 128]); s5d = inp("s5d", [DEPTH, 128, 1])
    w_small = inp("w_small", [DEPTH, 3, 256, 256])
    poolw = inp("poolw", [DEPTH, 128, 128]); poolsc = inp("poolsc", [DEPTH, 128, 1])
    convw = inp("convw", [DEPTH, 128, 31]); convb = inp("convb", [DEPTH, 128, 1]); lnT = inp("lnT", [DEPTH, 128, 2, 2])
    w_branch = inp("w_branch", [DEPTH, 4, 256, D]); w_out = inp("w_out", [DEPTH, D, D])
    w1 = inp("mlp_w1", [DEPTH, D, 4 * D]); w2 = inp("mlp_w2", [DEPTH, 4 * D, D])
    c_ident = inp("c_ident", [128, 128]); c_cs = inp("c_cs", [128, 128])
    c_w64 = inp("c_w64", [64, 2, 128])
    c_twL = inp("c_twL", [128, 2, 64]); c_twC = inp("c_twC", [4, 2, 64])
    c_fL = inp("c_fL", [128, 2, 128]); c_fC = inp("c_fC", [4, 2, 4])
    out = P.dram("out", [8, 128, NT], F32, kind="ExternalOutput")
    dbg_out = {}

    xres = P.dram("xres", [8, 128, NTK], F32)
    hres = P.dram("hres", [8, 128, NTK], BF16)
    hloc_c = [P.dram("hloc%d" % c, [256, NT], BF16) for c in range(4)]
    hg_c = [P.dram("hg%d" % c, [4 * 256, NT], BF16) for c in range(4)]
    yloc_c = [P.dram("yloc%d" % c, [32, SEQ], BF16) for c in range(8)]
    yg_c = [P.dram("ygc%d" % c, [4 * 32, SEQ], BF16) for c in range(8)]
    zown = P.dram("zown", [3, 128, SEQ], BF16)
    yown = P.dram("yown", [2 * 128, SEQ], BF16)
    ypre = P.dram("ypre", [8, 128, NTK], BF16)
    ypost = P.dram("ypost", [8, 128, NTK], BF16)
    mres = P.dram("mres", [8, 128, NTK], BF16)
    ares = P.dram("ares", [32, 128, NTK], BF16)
    GROUPS = [[0, 1, 2, 3], [4, 5, 6, 7]] if NCORES == 8 else [[0, 1, 2, 3]]

    ps_all = P.psum("ps", [128, 4096])
    banks = [P.view(ps_all[:, 512 * i:512 * (i + 1)], "bank%d" % i) for i in range(8)]
    bstate = [0]

    def bank(lo=0, hi=8):
        i = lo + bstate[0] % (hi - lo)
        bstate[0] += 1
        return banks[i]

    identf = P.sbuf("identf", [128, 128], F32)
    identb = P.sbuf("identb", [128, 128], BF16)
    onesb = P.sbuf("onesb", [128, 128], BF16)
    ones256 = P.sbuf("ones256", [128, 128], F32)
    epsc = P.sbuf("epsc", [128, 1], F32)
    halfpi = P.sbuf("halfpi", [128, 1], F32)
    mskt = P.sbuf("mskt", [128, 8], F32)
    cv = P.sbuf("cv", [128, 8, 2], F32)
    scv = P.sbuf("scv", [128, 8, 2], F32)
    modv = P.sbuf("modv", [128, 48, 2], F32)
    mulv = P.sbuf("mulv", [128, 2, 8, 2], F32)
    gfin = P.sbuf("gfin", [128, 8], F32)
    P.dma("sp", identf, c_ident)
    P.dma("pool", identb, c_ident)
    P.memset("dve", onesb, 1.0 / 1024)
    P.memset("dve", ones256, 1.0 / 256)
    P.memset("dve", epsc, EPS)
    P.memset("dve", halfpi, math.pi / 2)
    P.dma("sp", mskt, msk)
    P.dma("sp", cv, cvec)
    P.dma("sp", gfin, gfT)
    P.act(scv, cv, AF.Silu)

    def tiles_for(layer_last_skip_ctx):
        return [t for t in TOK if not (layer_last_skip_ctx and t[2] == 1)]

    def tile_view(tv, ti, kt=None):
        t0, tn, _ = TOK[ti]
        return P.dv(tv, ti, tv.ap[:, :, t0:t0 + tn])

    with P.scope():
        ra = Ring(P, "s0a", [128, NT], F32, 2)
        rb = Ring(P, "s0b", [128, NT], F32, 2)
        for dt in range(8):
            a = ra.next(); b = rb.next()
            P.dma("sp", a, xT[dt]); P.dma("act", b, posT[dt])
            P.tt("pool" if dt % 2 else "dve", a, a, b, ALU.add)
            P.dma("sp", P.dv(xres, ("s0", dt), xres.ap[dt, :, CT:NTK]), a)
        P.dma("pool", P.dv(xres, "s0c", xres.ap[:, :, 0:CT]), ctxT)

    def norm_stage(l, which, final=False):
        with P.scope():
            xr = Ring(P, "nx", [128, 8, 512], F32, 2)
            sqr = Ring(P, "nsq", [128, 8, 512], BF16, 1)
            tmpr = Ring(P, "ntmp", [128, 8, 512], F32, 1)
            hr = Ring(P, "nh", [128, 8, 512], F32 if final else BF16, 2)
            sdr = Ring(P, "nsd", [128, 512], F32, 2)
            for ti, (t0, tn, s) in enumerate(TOK):
                if final and s == 1:
                    continue
                xt = xr.next(); sq = sqr.next(); tmp = tmpr.next(); h = hr.next(); sd = sdr.next()
                P.dma("sp", xt[:, :, :tn], tile_view(xres, ti).rr("k p t -> p k t"))
                P.act(sq[:, :, :tn], xt[:, :, :tn], AF.Square)
                ps = bank()
                for kt in range(8):
                    P.mm(ps[:, :tn], onesb, sq[:, kt, :tn], start=kt == 0, stop=kt == 7)
                P.act(sd[:, :tn], ps[:, :tn], AF.Sqrt, bias=epsc)
                P.recip(sd[:, :tn], sd[:, :tn])
                P.tt("dve", tmp[:, :, :tn], xt[:, :, :tn], sd[:, None, :tn].bc([128, 8, tn]), ALU.mult)
                for kt in range(8):
                    if final:
                        P.act(h[:, kt, :tn], tmp[:, kt, :tn], AF.Identity, scale=gfin[:, kt:kt + 1])
                    else:
                        P.act(h[:, kt, :tn], tmp[:, kt, :tn], AF.Identity, scale=mulv[:, which, kt, s:s + 1],
                              bias=modv[:, (0 if which == 0 else 24) + kt, s:s + 1])
                if final:
                    P.dma("sp", P.dv(out, ti, out.ap[:, :, t0 - CT:t0 - CT + tn]).rr("k p t -> p k t"), h[:, :, :tn])
                else:
                    P.dma("pool", tile_view(hres, ti).rr("k p t -> p k t"), h[:, :, :tn])

    def mod_stage(l):
        with P.scope():
            wr = Ring(P, "wm", [128, 8, 512], F32, 2)
            bm = P.sbuf("bm", [128, 48], F32)
            g12 = P.sbuf("g12", [128, 2, 8], F32)
            P.dma("sp", bm, b_modT[l]); P.dma("sp", g12[:, 0, :], g1T[l]); P.dma("sp", g12[:, 1, :], g2T[l])
            ps = bank()
            wv = w_mod.ap[l].rearrange("(kt p) m -> p kt m", p=128)
            for ch in range(12):
                w = wr.next()
                P.dma("sp" if ch % 2 else "act", w, P.dv(w_mod, (l, ch), wv[:, :, 512 * ch:512 * (ch + 1)]))
                for o in range(4):
                    oc = ch * 4 + o
                    for kt in range(8):
                        P.mm(ps[:, 2 * oc:2 * oc + 2], w[:, kt, 128 * o:128 * (o + 1)], scv[:, kt, :],
                             start=kt == 0, stop=kt == 7)
            P.tt("dve", modv, ps[:, 0:96].rr("p (o s) -> p o s", s=2), bm[:, :, None].bc([128, 48, 2]), ALU.add)
            for which, base in ((0, 8), (1, 32)):
                P.ts("dve", mulv[:, which], modv[:, base:base + 8, :], 1.0, ALU.add)
                P.tt("dve", mulv[:, which], mulv[:, which], g12[:, which, :, None].bc([128, 8, 2]), ALU.mult)

    def dense_stage(name, wview, KT, M, src, tiles, evac, pre=None, post=None, src_fn=None):
        with P.scope():
            wsb = P.sbuf("w_" + name, [128, KT, M], BF16)
            cw = 512 if KT <= 8 else 128
            chunks = []
            for c0 in range(0, M, cw):
                v = P.view(wsb[:, :, c0:c0 + cw])
                P.dma("pool", v, TV(wview[:, :, c0:c0 + cw], Root("wsrc")))
                chunks.append(v)

            def wt(kt, mt):
                c = (mt * 128) // cw
                o = mt * 128 - c * cw
                return chunks[c][:, kt, o:o + 128]
            xr = Ring(P, "dx_" + name, [128, KT, 512], BF16, 2)
            for ti in tiles:
                t0, tn, s = TOK[ti]
                xin = xr.next()
                sv = src_fn(ti) if src_fn else tile_view(src, ti).rr("k p t -> p k t")
                P.dma("sp", xin[:, :, :tn], sv)
                st = pre(ti) if pre else None
                for mt in range(M // 128):
                    ps = bank(0, 6)
                    for kt in range(KT):
                        P.mm(ps[:, :tn], wt(kt, mt), xin[:, kt, :tn],
                             start=kt == 0, stop=kt == KT - 1)
                    evac(st, mt, ti, ps[:, :tn])
                if post:
                    post(st, ti)

    def zown_stage(l):
        with P.scope():
            for c in range(4):
                P.dma("sp" if c % 2 else "act", P.dv(hloc_c[c], "all", hloc_c[c].ap.rearrange("(k p) t -> k p t", p=128)),
                      P.dv(hres, ("lat", c), hres.ap[2 * c:2 * c + 2, :, CT:NTK]))
        with P.scope():
            for c in range(4):
                P.gather(hloc_c[c], hg_c[c], GROUPS)
        with P.scope():
            wsb = P.sbuf("w_zo", [128, 8, 384], BF16)
            P.dma("pool", wsb, P.dv(w_in_own, l, w_in_own.ap[l].rearrange("(kt p) m -> p kt m", p=128)))
            xr = Ring(P, "zx", [128, 8, 512], BF16, 2)
            zr = Ring(P, "zz", [128, 3, 512], BF16, 2)
            hgv = [hg_c[c].ap.rearrange("(r k p) t -> r p k t", r=4, k=2) for c in range(4)]
            for si in range(17):
                xin = xr.next(); z = zr.next()
                if si == 0:
                    tn = CT; s0 = 0
                    P.dma("sp", xin[:, :, :tn], tile_view(hres, 0).rr("k p t -> p k t"))
                else:
                    tn = 512; r = (si - 1) // 4; tt = (si - 1) % 4; s0 = CT + (si - 1) * 512
                    for c in range(4):
                        P.dma("sp" if c % 2 else "act", xin[:, 2 * c:2 * c + 2, :],
                              P.dv(hg_c[c], si, hgv[c][r, :, :, 512 * tt:512 * (tt + 1)]))
                for mt in range(3):
                    ps = bank()
                    for kt in range(8):
                        P.mm(ps[:, :tn], wsb[:, kt, 128 * mt:128 * (mt + 1)], xin[:, kt, :tn], start=kt == 0, stop=kt == 7)
                    if mt == 2:
                        P.act(z[:, mt, :tn], ps[:, :tn], AF.Sigmoid)
                    elif mt == 1:
                        P.copy("dve", z[:, mt, :tn], ps[:, :tn])
                    else:
                        P.copy("act", z[:, mt, :tn], ps[:, :tn])
                P.dma("pool", P.dv(zown, si, zown.ap[:, :, s0:s0 + tn]).rr("m p t -> p m t"), z[:, :, :tn])

    def s5_stage(l):
        T = TCH
        with P.scope():
            za = P.sbuf("za", [128, SEQ], BF16)
            P.dma("sp", za, P.dv(zown, "za", zown.ap[0]))
            lam = P.sbuf("lam", [128, 3, 4], F32)
            P.dma("sp", lam, s5lam[l])
            BT = P.sbuf("BT", [128, 8, 128], BF16)
            CTp = P.sbuf("CTp", [128, 8, 128], BF16)
            CTn = P.sbuf("CTn", [128, 8, 128], BF16)
            P.dma("pool", BT, s5BT[l]); P.dma("pool", CTp, s5CT[l])
            P.ts("pool", CTn, CTp, -1.0, ALU.mult)
            dcol = P.sbuf("dcol", [128, 1], F32)
            P.dma("sp", dcol, s5d[l])
            sm = {n: P.sbuf("s5_" + n, [128, 4], F32) for n in
                  ("dt", "mag", "ang", "c", "s", "t1", "t2", "are", "aim", "den", "fre", "fim", "ar1",
                   "etr", "eti", "inr", "ini", "u1", "u2", "u3", "u4")}
            P.act(sm["dt"], lam[:, 2, :], AF.Exp)
            P.tt("dve", sm["t1"], lam[:, 0, :], sm["dt"], ALU.mult)
            P.act(sm["mag"], sm["t1"], AF.Exp)
            P.tt("dve", sm["ang"], lam[:, 1, :], sm["dt"], ALU.mult)
            P.act(sm["s"], sm["ang"], AF.Sin, scale=1.0 / 32)
            P.act(sm["c"], sm["ang"], AF.Sin, scale=1.0 / 32, bias=halfpi)

            def csq(c, s):
                P.tt("dve", sm["t1"], c, c, ALU.mult)
                P.tt("dve", sm["t2"], s, s, ALU.mult)
                P.stt(s, c, 2.0, s, ALU.mult, ALU.mult)
                P.tt("dve", c, sm["t1"], sm["t2"], ALU.subtract)
            for _ in range(5):
                csq(sm["c"], sm["s"])
            P.tt("dve", sm["are"], sm["mag"], sm["c"], ALU.mult)
            P.tt("dve", sm["aim"], sm["mag"], sm["s"], ALU.mult)
            P.tt("dve", sm["t1"], lam[:, 0, :], lam[:, 0, :], ALU.mult)
            P.tt("dve", sm["t2"], lam[:, 1, :], lam[:, 1, :], ALU.mult)
            P.tt("dve", sm["den"], sm["t1"], sm["t2"], ALU.add)
            P.recip(sm["den"], sm["den"])
            P.ts("dve", sm["ar1"], sm["are"], -1.0, ALU.add)
            P.tt("dve", sm["t1"], sm["ar1"], lam[:, 0, :], ALU.mult)
            P.tt("dve", sm["t2"], sm["aim"], lam[:, 1, :], ALU.mult)
            P.tt("dve", sm["fre"], sm["t1"], sm["t2"], ALU.add)
            P.tt("dve", sm["fre"], sm["fre"], sm["den"], ALU.mult)
            P.tt("dve", sm["t1"], sm["aim"], lam[:, 0, :], ALU.mult)
            P.tt("dve", sm["t2"], sm["ar1"], lam[:, 1, :], ALU.mult)
            P.tt("dve", sm["fim"], sm["t1"], sm["t2"], ALU.subtract)
            P.tt("dve", sm["fim"], sm["fim"], sm["den"], ALU.mult)
            Er = P.sbuf("Er", [128, 4, T], F32); Ei = P.sbuf("Ei", [128, 4, T], F32)
            Rr = P.sbuf("Rr", [128, 4, T], F32); Ri = P.sbuf("Ri", [128, 4, T], F32)
            MAG0 = P.sbuf("MAG0", [128, 4, T], F32)
            tA = P.sbuf("tA", [128, 4, T], F32); tB = P.sbuf("tB", [128, 4, T], F32)
            P.memset("dve", Er[:, :, 0:1], 1.0); P.memset("dve", Ei[:, :, 0:1], 0.0)
            pc = P.sbuf("pc", [128, 4], F32); psn = P.sbuf("psn", [128, 4], F32)
            P.copy("dve", pc, sm["c"]); P.copy("dve", psn, sm["s"])
            s = 1
            while s < T:
                cb = pc[:, :, None].bc([128, 4, s]); sb = psn[:, :, None].bc([128, 4, s])
                P.tt("dve", tA[:, :, 0:s], Er[:, :, 0:s], cb, ALU.mult)
                P.tt("dve", tB[:, :, 0:s], Ei[:, :, 0:s], sb, ALU.mult)
                P.tt("dve", Er[:, :, s:2 * s], tA[:, :, 0:s], tB[:, :, 0:s], ALU.subtract)
                P.tt("dve", tA[:, :, 0:s], Er[:, :, 0:s], sb, ALU.mult)
                P.tt("dve", tB[:, :, 0:s], Ei[:, :, 0:s], cb, ALU.mult)
                P.tt("dve", Ei[:, :, s:2 * s], tA[:, :, 0:s], tB[:, :, 0:s], ALU.add)
                csq(pc, psn)
                s *= 2
            P.tt("dve", sm["etr"], pc, sm["mag"], ALU.mult)
            P.tt("dve", sm["eti"], psn, sm["mag"], ALU.mult)
            frb = sm["fre"][:, :, None].bc([128, 4, T]); fib = sm["fim"][:, :, None].bc([128, 4, T])
            P.tt("dve", tA, Er, frb, ALU.mult); P.tt("dve", tB, Ei, fib, ALU.mult)
            P.tt("dve", Rr, tA, tB, ALU.add)
            P.tt("dve", tA, Er, fib, ALU.mult); P.tt("dve", tB, Ei, frb, ALU.mult)
            P.tt("dve", Ri, tA, tB, ALU.subtract)
            P.copy("dve", MAG0, sm["mag"][:, :, None].bc([128, 4, T]))
            P.memset("dve", MAG0[:, :, 0:1], 0.0)
            P.memset("dve", sm["inr"], 0.0); P.memset("dve", sm["ini"], 0.0)

            ysum = P.sbuf("ysum", [128, SEQ], F32)
            NB = 2
            bus_l = [P.sbuf("bus", [128, 2, 4, T], F32) for _ in range(NB)]
            t_l = [[P.sbuf("t%d" % i, [128, 4, T], F32) for i in range(4)] for _ in range(NB)]
            m_l = [[P.sbuf("m%d" % i, [128, 4, T], F32) for i in range(2)] for _ in range(NB)]
            pr_l = [[P.sbuf("pr%d" % i, [128, 4, T], BF16) for i in range(4)] for _ in range(NB)]
            stepc = [0]
            BU = P.view(ps_all[:, 0:2048], "BU")
            ybk = [P.view(ps_all[:, 2048 + 512 * i:2048 + 512 * (i + 1)], "yb%d" % i) for i in range(4)]
            ycnt = [0]
            for (start, nch) in ((0, CT // T), (CT, L // T)):
                for ci in range(nch):
                    cf = start + ci * T
                    cbk = start + (nch - 1 - ci) * T
                    bi = stepc[0] % NB; stepc[0] += 1
                    bus = bus_l[bi]; t1, t2, t3, t4 = t_l[bi]; mre, mim = m_l[bi]; pr = pr_l[bi]
                    buv = BU.rr("p (r k t) -> p r k t", r=2, k=4)
                    for kq in range(4):
                        k = kq // 2
                        if k == 0:
                            rhs = za[:, cf:cf + T]
                        else:
                            rhs = za.re(za.ap[:, cbk:cbk + T][:, ::-1])
                        for ri in range(2):
                            P.mm(buv[:, ri, kq, :], BT[:, kq * 2 + ri, :], rhs)
                    P.copy("act", bus, buv)
                    P.tt("pool", t1, Rr, bus[:, 0], ALU.mult)
                    P.tt("pool", t2, Ri, bus[:, 1], ALU.mult)
                    P.tt("pool", t1, t1, t2, ALU.subtract)
                    P.tt("dve", t3, Rr, bus[:, 1], ALU.mult)
                    P.tt("dve", t4, Ri, bus[:, 0], ALU.mult)
                    P.tt("pool", t3, t3, t4, ALU.add)
                    P.tt("dve", t1[:, :, 0], t1[:, :, 0], sm["inr"], ALU.add)
                    P.tt("dve", t3[:, :, 0], t3[:, :, 0], sm["ini"], ALU.add)
                    P.scan(mre.rr("p k t -> p (k t)"), MAG0.rr("p k t -> p (k t)"), t1.rr("p k t -> p (k t)"))
                    P.scan(mim.rr("p k t -> p (k t)"), MAG0.rr("p k t -> p (k t)"), t3.rr("p k t -> p (k t)"))
                    lr = mre[:, :, T - 1]; li = mim[:, :, T - 1]
                    P.tt("dve", sm["u1"], sm["etr"], lr, ALU.mult)
                    P.tt("dve", sm["u2"], sm["eti"], li, ALU.mult)
                    P.tt("dve", sm["inr"], sm["u1"], sm["u2"], ALU.subtract)
                    P.tt("dve", sm["u3"], sm["eti"], lr, ALU.mult)
                    P.tt("dve", sm["u4"], sm["etr"], li, ALU.mult)
                    P.tt("dve", sm["ini"], sm["u3"], sm["u4"], ALU.add)
                    P.tt("pool", pr[0], Er, mre, ALU.mult)
                    P.tt("dve", pr[1], Ei, mim, ALU.mult)
                    P.tt("pool", pr[2], Ei, mre, ALU.mult)
                    P.tt("dve", pr[3], Er, mim, ALU.mult)
                    for k in range(2):
                        yb = ybk[ycnt[0] % 4]; ycnt[0] += 1
                        n = 0
                        for q in range(2):
                            kq = 2 * k + q
                            for pi, (tab, ri) in enumerate(((CTp, 0), (CTn, 0), (CTn, 1), (CTn, 1))):
                                rhs = pr[pi][:, kq, :]
                                if k == 1:
                                    rhs = pr[pi].re(pr[pi].ap[:, kq, :][:, ::-1])
                                P.mm(yb[:, :T], tab[:, kq * 2 + ri, :], rhs, start=n == 0, stop=n == 7)
                                n += 1
                        c0 = cf if k == 0 else cbk
                        first = (ci < nch - 1 - ci) if k == 0 else (nch - 1 - ci > ci)
                        if nch == 1:
                            first = (k == 0)
                        elif ci == nch - 1 - ci:
                            first = False
                        if first:
                            P.copy("act", ysum[:, c0:c0 + T], yb[:, :T])
                        else:
                            P.tt("dve", ysum[:, c0:c0 + T], ysum[:, c0:c0 + T], yb[:, :T], ALU.add)
            ya = P.sbuf("ya", [128, SEQ // 4], F32); yb2 = P.sbuf("yb2", [128, SEQ // 4], F32)
            yo = P.sbuf("yo", [128, SEQ // 4], BF16)
            W = SEQ // 4
            for c in range(4):
                sl = slice(c * W, (c + 1) * W)
                P.stt(ya, za[:, sl], dcol[:, 0:1], ysum[:, sl], ALU.mult, ALU.add)
                P.tt("pool", yb2, ya, ya, ALU.mult)
                P.ts("pool", yb2, yb2, 0.044715, ALU.mult, 1.0, ALU.add)
                P.tt("pool", yb2, yb2, ya, ALU.mult)
                P.act(yb2, yb2, AF.Sigmoid, scale=2.0 * math.sqrt(2.0 / math.pi))
                P.tt("dve", yo, ya, yb2, ALU.mult)
                P.dma("sp", P.dv(yown, ("s5", c), yown.ap[0:64, sl]), yo[0:64, :])

    def fnet_stage(l):
        with P.scope():
            za = P.sbuf("za", [128, SEQ], BF16)
            P.dma("sp", za, P.dv(zown, "za", zown.ap[0]))
            cs = P.sbuf("cs", [128, 128], BF16); P.dma("pool", cs, c_cs)
            w64 = P.sbuf("w64", [64, 2, 128], BF16); P.dma("pool", w64, c_w64)
            for (start, TF, ctw, cf) in ((0, 4, c_twC, c_fC), (CT, 128, c_twL, c_fL)):
              with P.scope():
                Lq = 64 * TF
                tw = P.sbuf("tw%d" % TF, [TF, 2, 64], F32); P.dma("sp", tw, ctw)
                fb = P.sbuf("fb%d" % TF, [TF, 2, TF], BF16); P.dma("pool", fb, cf)
                X = P.sbuf("X%d" % TF, [64, TF, 128], BF16)
                A = P.sbuf("A%d" % TF, [TF, 64, 128], F32)
                Ar = P.sbuf("Ar%d" % TF, [TF, 64, 64], BF16); Ai = P.sbuf("Ai%d" % TF, [TF, 64, 64], BF16)
                u1 = P.sbuf("u1%d" % TF, [TF, 64, 64], F32); u2 = P.sbuf("u2%d" % TF, [TF, 64, 64], F32)
                Y2 = P.sbuf("Y2%d" % TF, [TF, 64, 128], BF16)
                yo = P.sbuf("fyo%d" % TF, [128, Lq], BF16)
                P.memset("pool", Y2, 0.0)
                for g0 in range(0, TF, 4):
                    ps = bank()
                    for tf in range(g0, g0 + 4):
                        lhs = za.re(za.ap[:, start + tf:start + tf + TF * 63 + 1:TF])
                        P.mm(ps[0:64, 128 * (tf - g0):128 * (tf - g0 + 1)], lhs, cs)
                    P.copy("act" if (g0 // 4) % 2 else "dve", X[:, g0:g0 + 4, :], ps[0:64, :].rr("p (a c) -> p a c", a=4))
                for m0 in range(0, 64, 4):
                    ps = bank()
                    for m in range(m0, m0 + 4):
                        o = ps[0:TF, 128 * (m - m0):128 * (m - m0 + 1)]
                        P.mm(o, X[:, :, m], w64[:, 0, :], start=True, stop=False)
                        P.mm(o, X[:, :, 64 + m], w64[:, 1, :], start=False, stop=True)
                    P.copy("act" if (m0 // 4) % 2 else "dve", A[:, m0:m0 + 4, :], ps[0:TF, :].rr("p (a c) -> p a c", a=4))
                twr = tw[:, 0:1, :].bc([TF, 64, 64]); twi = tw[:, 1:2, :].bc([TF, 64, 64])
                P.tt("dve", u1, A[:, :, 0:64], twr, ALU.mult)
                P.tt("pool", u2, A[:, :, 64:128], twi, ALU.mult)
                P.tt("dve", Ar, u1, u2, ALU.add)
                P.tt("dve", u1, A[:, :, 0:64], twi, ALU.mult)
                P.tt("pool", u2, A[:, :, 64:128], twr, ALU.mult)
                P.tt("dve", Ai, u1, u2, ALU.subtract)
                scale = 1.0 / math.sqrt(64.0 * Lq)
                for blk in range(8):
                    ps = bank()
                    P.mm(ps[0:TF, :], fb[:, 0, :], Ar[:, 8 * blk:8 * blk + 8, :].rr("p m k -> p (m k)"), start=True, stop=False)
                    P.mm(ps[0:TF, :], fb[:, 1, :], Ai[:, 8 * blk:8 * blk + 8, :].rr("p m k -> p (m k)"), start=False, stop=True)
                    P.act(Y2[:, :, 64 + 8 * blk:64 + 8 * blk + 8].rr("p k m -> p m k"),
                          ps[0:TF, :].rr("p (m k) -> p m k", m=8), AF.Identity, scale=scale)
                yov = yo.ap.rearrange("p (kb ka) -> p ka kb", ka=64)
                per = max(1, 512 // TF)
                for ka0 in range(0, 64, per):
                    ps = bank()
                    nk = min(per, 64 - ka0)
                    for ka in range(ka0, ka0 + nk):
                        P.mm(ps[:, TF * (ka - ka0):TF * (ka - ka0 + 1)], Y2[:, ka, :], identb[0:TF, 0:TF])
                    P.copy("act" if (ka0 // per) % 2 else "dve", yo.re(yov[64:128, ka0:ka0 + nk, :]),
                           ps[64:128, 0:nk * TF].rr("p (a b) -> p a b", a=nk))
                P.dma("sp", P.dv(yown, ("fn", TF), yown.ap[64:128, start:start + Lq]), yo[64:128, :])

    def poolconv_stage(l):
        with P.scope():
            zb = P.sbuf("zb", [128, SEQ], BF16); zc = P.sbuf("zc", [128, SEQ], BF16)
            P.dma("sp", zb, P.dv(zown, "zb", zown.ap[1])); P.dma("sp", zc, P.dv(zown, "zc", zown.ap[2]))
            pw = P.sbuf("pw", [128, 128], BF16); P.dma("pool", pw, poolw[l])
            psc = P.sbuf("psc", [128, 1], F32); P.dma("sp", psc, poolsc[l])
            cw = P.sbuf("cw", [128, 31], F32); P.dma("sp", cw, convw[l])
            cb = P.sbuf("cb", [128, 1], F32); P.dma("sp", cb, convb[l])
            pfx = P.sbuf("pfx", [128, 16], F32); P.dma("sp", pfx, poolfix)
            Dk = P.sbuf("Dk", [128, 31, 128], BF16)
            for k in range(31):
                P.ts("pool", Dk[:, k, :], identf, cw[:, k:k + 1], ALU.mult)
            vx = P.sbuf("vx", [128, SEQ + 64], BF16)
            P.memset("pool", vx, 0.0)
            offs = {0: 16, CT: 16 + CT + 32}
            P.tt("dve", vx[:, 16:16 + CT], zb[:, 0:CT], zc[:, 0:CT], ALU.mult)
            P.tt("dve", vx[:, offs[CT]:offs[CT] + L], zb[:, CT:SEQ], zc[:, CT:SEQ], ALU.mult)
            yo = Ring(P, "cyo", [128, 512], BF16, 2)
            for si in range(17):
                if si == 0:
                    s0 = 0; tn = CT; e0 = 16
                else:
                    s0 = CT + (si - 1) * 512; tn = 512; e0 = offs[CT] + (si - 1) * 512
                ps = bank()
                for k in range(31):
                    P.mm(ps[:, :tn], Dk[:, k, :], vx[:, e0 + k - 15:e0 + k - 15 + tn], start=k == 0, stop=k == 30)
                y = yo.next()
                P.act(y[:, :tn], ps[:, :tn], AF.Identity, bias=cb[:, 0:1])
                P.dma("sp", P.dv(yown, ("cv", si), yown.ap[128 + 64:256, s0:s0 + tn]), y[64:128, :tn])
            CH = 2048
            ue = P.sbuf("ue", [128, CH + 32], F32)
            sa = P.sbuf("sa", [128, CH + 32], F32); sb_ = P.sbuf("sb", [128, CH + 32], F32)
            acc = P.sbuf("acc", [128, CH], F32)
            yp = P.sbuf("yp", [128, CH], BF16)
            po = Ring(P, "po", [128, 512], BF16, 2)
            P.memset("pool", yp, 0.0)
            for (start, Lq) in ((0, CT), (CT, L)):
                nchk = max(1, Lq // CH)
                cl = min(CH, Lq)
                for c in range(nchk):
                    c0 = start + c * cl
                    lo = 16 if c == 0 else 0
                    hi = 16 if c == nchk - 1 else 0
                    if lo or hi:
                        P.memset("pool", ue, 0.0)
                    P.copy("pool", ue[0:64, lo:cl + 32 - hi], zb[0:64, c0 - 16 + lo:c0 + cl + 16 - hi])
                    n = cl + 32
                    P.tt("pool", sa[0:64, 0:n - 1], ue[0:64, 0:n - 1], ue[0:64, 1:n], ALU.add)
                    P.ts("dve", acc[0:64, :cl], sa[0:64, 15:15 + cl], mskt[0:64, 4:5], ALU.mult)
                    P.tt("pool", sb_[0:64, 0:n - 3], sa[0:64, 0:n - 3], sa[0:64, 2:n - 1], ALU.add)
                    P.stt(acc[0:64, :cl], sb_[0:64, 14:14 + cl], mskt[0:64, 5:6], acc[0:64, :cl], ALU.mult, ALU.add)
                    P.tt("pool", sa[0:64, 0:n - 7], sb_[0:64, 0:n - 7], sb_[0:64, 4:n - 3], ALU.add)
                    P.stt(acc[0:64, :cl], sa[0:64, 12:12 + cl], mskt[0:64, 6:7], acc[0:64, :cl], ALU.mult, ALU.add)
                    P.tt("pool", sb_[0:64, 0:n - 15], sa[0:64, 0:n - 15], sa[0:64, 8:n - 7], ALU.add)
                    P.stt(acc[0:64, :cl], sb_[0:64, 8:8 + cl], mskt[0:64, 7:8], acc[0:64, :cl], ALU.mult, ALU.add)
                    if c == 0:
                        P.tt("dve", acc[0:64, 0:8], acc[0:64, 0:8], pfx[0:64, 0:8], ALU.mult)
                    if c == nchk - 1:
                        P.tt("dve", acc[0:64, cl - 8:cl], acc[0:64, cl - 8:cl], pfx[0:64, 8:16], ALU.mult)
                    P.tt("dve", yp[0:64, :cl], acc[0:64, :cl], ue[0:64, 16:16 + cl], ALU.subtract)
                    for t in range(0, cl, 512):
                        tn = min(512, cl - t)
                        ps = bank()
                        P.mm(ps[:, :tn], pw, yp[:, t:t + tn])
                        o = po.next()
                        P.act(o[0:64, :tn], ps[0:64, :tn], AF.Identity, scale=psc[0:64, 0:1])
                        P.dma("sp", P.dv(yown, ("pl", c0 + t), yown.ap[128:128 + 64, c0 + t:c0 + t + tn]), o[0:64, :tn])

    def select_stage(l):
        with P.scope():
            for c in range(8):
                P.dma("sp" if c % 2 else "act", yloc_c[c], P.dv(yown, ("cp", c), yown.ap[32 * c:32 * c + 32, :]))
        with P.scope():
            for c in range(8):
                P.gather(yloc_c[c], yg_c[c], GROUPS)
        with P.scope():
            ygv = [yg_c[c].ap.rearrange("(r i) s -> r i s", r=4) for c in range(8)]
            cr = Ring(P, "cand", [128, 4, NT], BF16, 2)
            orr = Ring(P, "selo", [128, NT], BF16, 2)
            tmp = P.sbuf("seltmp", [128, NT], F32)
            for br in range(4):
                T_ = 0 if br < 2 else 1
                ph = 0 if br in (0, 2) else 64
                for ct in range(2):
                    cand = cr.next(); o = orr.next()
                    for rl in range(2):
                        r = 2 * ct + rl
                        for hh in range(2):
                            c = (T_ * 128 + ph) // 32 + hh
                            p0 = 64 * rl + 32 * hh
                            P.dma("sp" if hh else "act", cand[p0:p0 + 32],
                                  P.dv(yg_c[c], (br, ct, rl), ygv[c][r, :, CT:SEQ]).rr("p (g t) -> p g t", g=4))
                            P.dma("pool", P.dv(ypre, ("c", br, ct, rl, hh), ypre.ap[br * 2 + ct, p0:p0 + 32, 0:CT]),
                                  P.dv(yg_c[c], ("c", br, ct, rl), ygv[c][r, :, 0:CT]))
                    P.ts("dve", tmp, cand[:, 0, :], mskt[:, 0:1], ALU.mult)
                    P.stt(tmp, cand[:, 1, :], mskt[:, 1:2], tmp, ALU.mult, ALU.add)
                    P.stt(tmp, cand[:, 2, :], mskt[:, 2:3], tmp, ALU.mult, ALU.add)
                    P.stt(o, cand[:, 3, :], mskt[:, 3:4], tmp, ALU.mult, ALU.add)
                    P.dma("pool", P.dv(ypre, ("l", br, ct), ypre.ap[br * 2 + ct, :, CT:NTK]), o)

    def post_stage(l, tiles):
        with P.scope():
            ws = P.sbuf("wsmall", [128, 3, 2, 256], BF16)
            P.dma("pool", ws, P.dv(w_small, l, w_small.ap[l].rearrange("w (kt p) m -> p w kt m", p=128)))
            ln = P.sbuf("ln", [128, 2, 2], F32); P.dma("sp", ln, lnT[l])
            yr = Ring(P, "py", [128, 8, 512], BF16, 2)
            orr = Ring(P, "pout", [128, 8, 512], BF16, 2)
            sg = P.sbuf("psg", [128, 512], F32)
            ycf = P.sbuf("ycf", [128, 2, 512], F32); xc = P.sbuf("xc", [128, 2, 512], F32)
            sq = P.sbuf("psq", [128, 2, 512], F32); sd = P.sbuf("psd", [128, 512], F32)
            yl = P.sbuf("yl", [128, 2, 512], BF16)
            for ti in tiles:
                t0, tn, s = TOK[ti]
                y = yr.next(); o = orr.next()
                P.dma("sp", y[:, :, :tn], tile_view(ypre, ti).rr("k p t -> p k t"))
                for mt in range(2):
                    ps = bank()
                    for kt in range(2):
                        P.mm(ps[:, :tn], ws[:, 0, kt, 128 * mt:128 * mt + 128], y[:, kt, :tn], start=kt == 0, stop=kt == 1)
                    P.act(sg[:, :tn], ps[:, :tn], AF.Sigmoid)
                    P.tt("dve", o[:, mt, :tn], y[:, mt, :tn], sg[:, :tn], ALU.mult)
                for mt in range(2):
                    ps = bank()
                    for kt in range(2):
                        P.mm(ps[:, :tn], ws[:, 1, kt, 128 * mt:128 * mt + 128], y[:, 2 + kt, :tn], start=kt == 0, stop=kt == 1)
                    P.copy("act", o[:, 2 + mt, :tn], ps[:, :tn])
                P.copy("pool", o[:, 4:6, :tn], y[:, 4:6, :tn])
                P.copy("pool", ycf[:, :, :tn], y[:, 6:8, :tn])
                ps = bank()
                for kt in range(2):
                    P.mm(ps[:, :tn], ones256, ycf[:, kt, :tn], start=kt == 0, stop=kt == 1)
                for kt in range(2):
                    P.tt("dve", xc[:, kt, :tn], ycf[:, kt, :tn], ps[:, :tn], ALU.subtract)
                P.act(sq[:, :, :tn], xc[:, :, :tn], AF.Square)
                ps2 = bank()
                for kt in range(2):
                    P.mm(ps2[:, :tn], ones256, sq[:, kt, :tn], start=kt == 0, stop=kt == 1)
                P.act(sd[:, :tn], ps2[:, :tn], AF.Sqrt, bias=epsc)
                P.recip(sd[:, :tn], sd[:, :tn])
                P.tt("dve", xc[:, :, :tn], xc[:, :, :tn], sd[:, None, :tn].bc([128, 2, tn]), ALU.mult)
                for kt in range(2):
                    P.act(yl[:, kt, :tn], xc[:, kt, :tn], AF.Silu, scale=ln[:, kt, 0:1], bias=ln[:, kt, 1:2])
                for mt in range(2):
                    ps = bank()
                    for kt in range(2):
                        P.mm(ps[:, :tn], ws[:, 2, kt, 128 * mt:128 * mt + 128], yl[:, kt, :tn], start=kt == 0, stop=kt == 1)
                    P.copy("act", o[:, 6 + mt, :tn], ps[:, :tn])
                P.dma("pool", tile_view(ypost, ti).rr("k p t -> p k t"), o[:, :, :tn])

    def merge_stage(l, tiles):
        wb = [None]
        yr = [None]
        accr = [None]
        mo = [None]

        def pre(ti):
            t0, tn, s = TOK[ti]
            if wb[0] is None:
                wb[0] = P.sbuf("wbr", [128, 4, 2, D], BF16)
                P.dma("pool", wb[0], P.dv(w_branch, l, w_branch.ap[l].rearrange("b (kt p) m -> p b kt m", p=128)))
                yr[0] = Ring(P, "my", [128, 8, 512], BF16, 2)
                accr[0] = [P.sbuf("macc", [128, 512], F32), P.sbuf("mtmp", [128, 512], F32), P.sbuf("msg", [128, 512], F32)]
                mo[0] = Ring(P, "mo", [128, 8, 512], BF16, 2)
            y = yr[0].next()
            P.dma("act", y[:, :, :tn], tile_view(ypost, ti).rr("k p t -> p k t"))
            return {"y": y, "o": mo[0].next()}

        def evac(st, mt, ti, ps):
            t0, tn, s = TOK[ti]
            dt, br = mt // 4, mt % 4
            acc, tmp, sg = accr[0]
            pp = bank(6, 8)
            for kt in range(2):
                P.mm(pp[:, :tn], wb[0][:, br, kt, 128 * dt:128 * dt + 128], st["y"][:, 2 * br + kt, :tn], start=kt == 0, stop=kt == 1)
            P.act(sg[:, :tn], ps, AF.Sigmoid)
            if br == 0:
                P.tt("dve", acc[:, :tn], pp[:, :tn], sg[:, :tn], ALU.mult)
            else:
                P.tt("dve", tmp[:, :tn], pp[:, :tn], sg[:, :tn], ALU.mult)
                if br < 3:
                    P.tt("pool", acc[:, :tn], acc[:, :tn], tmp[:, :tn], ALU.add)
                else:
                    P.tt("pool", st["o"][:, dt, :tn], acc[:, :tn], tmp[:, :tn], ALU.add)

        def post(st, ti):
            t0, tn, s = TOK[ti]
            P.dma("pool", tile_view(mres, ti).rr("k p t -> p k t"), st["o"][:, :, :tn])

        wv = w_in.ap[l].rearrange("(kt p) m -> p kt m", p=128)[:, :, 1280:5376]
        wv = wv.rearrange("p kt (b d c) -> p kt d b c", b=4, d=8)
        with P.scope():
            wsb = P.sbuf("w_gate", [128, 8, 8, 4, 128], BF16)
            gch = []
            for dt in range(8):
                for br in range(4):
                    v = P.view(wsb[:, :, dt, br])
                    P.dma("pool", v, TV(wv[:, :, dt, br], Root("wsrc")))
                    gch.append(v)
            xr = Ring(P, "dx_gate", [128, 8, 512], BF16, 2)
            for ti in tiles:
                t0, tn, s = TOK[ti]
                xin = xr.next()
                P.dma("sp", xin[:, :, :tn], tile_view(hres, ti).rr("k p t -> p k t"))
                st = pre(ti)
                for mt in range(32):
                    ps = bank(0, 6)
                    for kt in range(8):
                        P.mm(ps[:, :tn], gch[mt][:, kt, :], xin[:, kt, :tn], start=kt == 0, stop=kt == 7)
                    evac(st, mt, ti, ps[:, :tn])
                post(st, ti)

    def resid_stage(name, wview, KT, src, tiles, gate_base):
        xr = [None]

        def pre(ti):
            t0, tn, s = TOK[ti]
            if xr[0] is None:
                xr[0] = Ring(P, "rx_" + name, [128, 8, 512], F32, 2)
            x = xr[0].next()
            P.dma("act", x[:, :, :tn], tile_view(xres, ti).rr("k p t -> p k t"))
            return x

        def evac(x, mt, ti, ps):
            t0, tn, s = TOK[ti]
            P.stt(x[:, mt, :tn], ps, modv[:, gate_base + mt, s:s + 1], x[:, mt, :tn], ALU.mult, ALU.add)

        def post(x, ti):
            t0, tn, s = TOK[ti]
            P.dma("pool", tile_view(xres, ti).rr("k p t -> p k t"), x[:, :, :tn])
        dense_stage(name, wview, KT, D, src, tiles, evac, pre, post)

    def mlp1_stage(l, tiles):
        ar = [None]

        def pre(ti):
            if ar[0] is None:
                ar[0] = (Ring(P, "m1a", [128, 32, 512], BF16, 2), Ring(P, "m1r", [128, 512], F32, 3))
            return ar[0][0].next()

        def evac(a, mt, ti, ps):
            t0, tn, s = TOK[ti]
            r = ar[0][1].next()
            P.act(r[:, :tn], ps, AF.Relu)
            P.tt("pool", a[:, mt, :tn], r[:, :tn], r[:, :tn], ALU.mult)

        def post(a, ti):
            t0, tn, s = TOK[ti]
            P.dma("pool", tile_view(ares, ti).rr("k p t -> p k t"), a[:, :, :tn])
        dense_stage("w1", w1.ap[l].rearrange("(kt p) m -> p kt m", p=128), 8, 4 * D, hres, tiles, evac, pre, post)

    sched = []
    for l in range(DEPTH):
        last = l == DEPTH - 1
        tl = [i for i, t in enumerate(TOK) if not (last and t[2] == 1)]
        sched += [
            ("mod%d" % l, lambda l=l: mod_stage(l)),
            ("norm1_%d" % l, lambda l=l: norm_stage(l, 0)),
            ("zown%d" % l, lambda l=l: zown_stage(l)),
            ("s5_%d" % l, lambda l=l: s5_stage(l)),
            ("fnet%d" % l, lambda l=l: fnet_stage(l)),
            ("poolconv%d" % l, lambda l=l: poolconv_stage(l)),
            ("select%d" % l, lambda l=l: select_stage(l)),
            ("post%d" % l, lambda l=l, tl=tl: post_stage(l, tl)),
            ("merge%d" % l, lambda l=l, tl=tl: merge_stage(l, tl)),
            ("wo%d" % l, lambda l=l, tl=tl: resid_stage("wo", w_out.ap[l].rearrange("(kt p) m -> p kt m", p=128), 8, mres, tl, 16)),
            ("norm2_%d" % l, lambda l=l: norm_stage(l, 1)),
            ("mlp1_%d" % l, lambda l=l, tl=tl: mlp1_stage(l, tl)),
            ("w2_%d" % l, lambda l=l, tl=tl: resid_stage("w2", w2.ap[l].rearrange("(kt p) m -> p kt m", p=128), 32, ares, tl, 40)),
        ]
    sched.append(("final", lambda: norm_stage(DEPTH - 1, 0, final=True)))
    scratch = dict(xres=xres, hres=hres, zown=zown, yown=yown, ypre=ypre, ypost=ypost, mres=mres, ares=ares)
    for name, fn in sched:
        fn()
        if upto is not None and name == upto:
            break
    if dbg:
        with P.scope():
            for nm in dbg:
                if nm == "modv":
                    d = P.dram("dbg_modv", [128, 96], F32, kind="ExternalOutput")
                    P.dma("sp", d, modv.rr("p o s -> p (o s)"))
                    continue
                src = scratch[nm]
                shp = list(src.ap.shape)
                d = P.dram("dbg_" + nm, shp, src.ap.dtype, kind="ExternalOutput")
                P.dma("sp", d, P.dv(src, "dbg"))
    P.finish()
    return nc, P


def _consts():
    c = {}
    c["c_ident"] = np.eye(128, dtype=np.float32)
    j = np.arange(64)
    m = np.arange(64)
    cs = np.zeros((128, 128), np.float32)
    ang = 2 * np.pi * np.outer(j, m) / 64
    cs[64:, 0:64] = np.cos(ang)
    cs[64:, 64:128] = np.sin(ang)
    c["c_cs"] = cs
    ts = np.arange(64); ka = np.arange(64)
    a = 2 * np.pi * np.outer(ts, ka) / 64
    w64 = np.zeros((64, 2, 128), np.float32)
    w64[:, 0, 0:64] = np.cos(a); w64[:, 0, 64:] = np.sin(a)
    w64[:, 1, 0:64] = -np.sin(a); w64[:, 1, 64:] = np.cos(a)
    c["c_w64"] = w64
    for nm, TF in (("L", 128), ("C", 4)):
        tf = np.arange(TF)
        ph = 2 * np.pi * np.outer(tf, ka) / (64 * TF)
        tw = np.zeros((TF, 2, 64), np.float32)
        tw[:, 0] = np.cos(ph); tw[:, 1] = -np.sin(ph)
        c["c_tw" + nm] = tw
        a2 = 2 * np.pi * np.outer(tf, np.arange(TF)) / TF
        f = np.zeros((TF, 2, TF), np.float32)
        f[:, 0] = np.cos(a2); f[:, 1] = np.sin(a2)
        c["c_f" + nm] = f
    return c


def _pos_table():
    rows = L // 64
    row = np.repeat(np.arange(rows), 64).astype(np.float32)
    col = np.tile(np.arange(64), rows).astype(np.float32)
    q = D // 4
    freq = (1.0 / (10000.0 ** (np.arange(q, dtype=np.float32) / q))).astype(np.float32)

    def enc(p):
        a = (p[:, None] * freq[None, :]).astype(np.float32)
        return np.concatenate([np.sin(a), np.cos(a)], axis=-1)
    return np.concatenate([enc(row), enc(col)], axis=-1).astype(np.float32)


def _fm(a):
    return np.ascontiguousarray(a.T.reshape(8, 128, a.shape[0]))


def _col(v):
    return np.ascontiguousarray(v.reshape(-1, 128).T)


def make_inputs(inp):
    f = lambda k: np.asarray(inp[k], dtype=np.float32)
    x, c, ctx, c_ctx = f("x"), f("c"), f("ctx"), f("c_ctx")
    pos = _pos_table()
    consts = _consts()
    shared = dict(consts)
    shared["w_mod"] = f("w_mod")
    shared["b_modT"] = np.stack([_col(f("b_mod")[l]) for l in range(DEPTH)])
    shared["g1T"] = np.stack([_col(f("g_norm1")[l]) for l in range(DEPTH)])
    shared["g2T"] = np.stack([_col(f("g_norm2")[l]) for l in range(DEPTH)])
    shared["gfT"] = _col(f("g_final"))
    shared["w_in"] = f("w_in")
    shared["w_small"] = np.stack([np.stack([f("s5_w_glu")[l], f("fnet_w")[l], f("conv_w_out")[l]]) for l in range(DEPTH)])
    shared["lnT"] = np.stack([np.stack([_col(f("conv_ln_g")[l]), _col(f("conv_ln_b")[l])], axis=-1) for l in range(DEPTH)])
    shared["w_branch"] = f("w_branch"); shared["w_out"] = f("w_out")
    shared["mlp_w1"] = f("mlp_w1"); shared["mlp_w2"] = f("mlp_w2")
    win = f("w_in")
    lam_re, lam_im, log_dt = f("s5_lam_re"), f("s5_lam_im"), f("s5_log_dt")
    b_re, b_im, c_re, c_im = f("s5_b_re"), f("s5_b_im"), f("s5_c_re"), f("s5_c_im")
    maps = []
    for core in range(8):
        b, j = core // 4, core % 4
        m = dict(shared)
        m["xT"] = _fm(x[b, NT * j:NT * (j + 1)])
        m["posT"] = _fm(pos[NT * j:NT * (j + 1)])
        m["ctxT"] = _fm(ctx[b])
        m["cvec"] = np.ascontiguousarray(np.stack([_col(c[b]), _col(c_ctx)], axis=-1))
        mk = np.zeros((128, 8), np.float32)
        mk[:, j] = 1.0
        wj = [2, 4, 8, 16][j]
        mk[:, 4 + j] = 1.0 / wj
        m["msk"] = mk
        pf = np.ones((128, 16), np.float32)
        for t in range(8):
            lo = max(t - wj // 2, 0); hi = t + wj // 2
            pf[:, t] = wj / float(hi - lo)
            e = 8 - t
            cnt = min(wj // 2, e) + wj // 2
            pf[:, 8 + t] = wj / float(cnt)
        m["poolfix"] = pf
        wo = np.zeros((DEPTH, D, 384), np.float32)
        for l in range(DEPTH):
            wo[l, :, 0:64] = win[l][:, 64 * j:64 * j + 64]
            wo[l, :, 64:128] = win[l][:, 256 + 64 * j:256 + 64 * j + 64]
            wo[l, :, 128:192] = win[l][:, 512 + 64 * j:512 + 64 * j + 64]
            wo[l, :, 192:256] = win[l][:, 768 + 64 * j:768 + 64 * j + 64]
            wo[l, :, 320:384] = win[l][:, 1024 + 64 * j:1024 + 64 * j + 64]
        m["w_in_own"] = wo
        lamt = np.zeros((DEPTH, 128, 3, 4), np.float32)
        BT = np.zeros((DEPTH, 128, 8, 128), np.float32)
        CTt = np.zeros((DEPTH, 128, 8, 128), np.float32)
        dd = np.zeros((DEPTH, 128, 1), np.float32)
        for l in range(DEPTH):
            dd[l, 0:64, 0] = f("s5_d")[l][64 * j:64 * j + 64]
            for k in range(2):
                for q in range(2):
                    kq = 2 * k + q
                    for gl in range(2):
                        g = 4 * j + 2 * q + gl
                        pr = slice(64 * gl, 64 * gl + 64)
                        lamt[l, pr, 0, kq] = lam_re[l, k, g]
                        lamt[l, pr, 1, kq] = lam_im[l, k, g]
                        lamt[l, pr, 2, kq] = log_dt[l, k, g]
                        ch = slice(32 * q + 16 * gl, 32 * q + 16 * gl + 16)
                        BT[l, ch, kq * 2 + 0, pr] = b_re[l, k, g].T
                        BT[l, ch, kq * 2 + 1, pr] = b_im[l, k, g].T
                        CTt[l, pr, kq * 2 + 0, ch] = c_re[l, k, g].T
                        CTt[l, pr, kq * 2 + 1, ch] = c_im[l, k, g].T
        m["s5lam"] = lamt; m["s5BT"] = BT; m["s5CT"] = CTt; m["s5d"] = dd
        pw = np.zeros((DEPTH, 128, 128), np.float32)
        psc = np.zeros((DEPTH, 128, 1), np.float32)
        cwt = np.zeros((DEPTH, 128, 31), np.float32)
        cbt = np.zeros((DEPTH, 128, 1), np.float32)
        for l in range(DEPTH):
            pw[l, 0:64, 0:64] = f("pool_w")[l, j]
            psc[l, 0:64, 0] = f("pool_scale")[l][64 * j:64 * j + 64]
            cwt[l, 64:128, :] = f("conv_w")[l][:, 64 * j:64 * j + 64].T
            cbt[l, 64:128, 0] = f("conv_b")[l][64 * j:64 * j + 64]
        m["poolw"] = pw; m["poolsc"] = psc; m["convw"] = cwt; m["convb"] = cbt
        maps.append({k: np.ascontiguousarray(v, dtype=np.float32) for k, v in m.items()})
    return maps


_NC = [None]


def kernel(**inputs):
    if _NC[0] is None:
        _NC[0] = build()[0]
    nc = _NC[0]
    maps = make_inputs(inputs)
    res = run_bass_kernel_spmd(nc, maps, core_ids=list(range(8)))
    outp = np.zeros((2, L, D), np.float32)
    for core in range(8):
        b, j = core // 4, core % 4
        o = np.asarray(res.results[core]["out"], dtype=np.float32)
        outp[b, NT * j:NT * (j + 1)] = o.reshape(D, NT).T
    return outp
```
